# Optimizing a Trainium2 kernel written in Bass

```python
import math
import jax, jax.numpy as jnp
from jax import lax
import numpy as np

D_MODEL = 1024
BATCH = 8
SEQ = 2048
DEPTH = 2

HEAD_DIM = 64
BLOCK = 128
WINDOW = 128
GRID_W = 64
A_HEADS = 6
A_KV_HEADS = 2
B_HEADS = 6
B_KV_HEADS = 2
C_HEADS = 6
C_V_DIM = 2 * HEAD_DIM
X_HEADS = 4
MEM_LEN = 256
REL_HEADS = 6
REL_BUCKETS = 32
REL_MAX_DIST = 128
ROPE_THETA = 10000.0
EPS = 1e-6
NEG_INF = -1e30
N_EVEN = (DEPTH + 1) // 2
N_ODD = DEPTH // 2

EVEN_MIX = (A_HEADS + B_HEADS + X_HEADS) * HEAD_DIM
ODD_MIX = C_HEADS * C_V_DIM + X_HEADS * HEAD_DIM
EVEN_SPLITS = (A_HEADS * HEAD_DIM, A_KV_HEADS * HEAD_DIM, A_KV_HEADS * HEAD_DIM,
               B_HEADS * HEAD_DIM, B_KV_HEADS * HEAD_DIM, B_KV_HEADS * HEAD_DIM,
               X_HEADS * HEAD_DIM, EVEN_MIX)
ODD_SPLITS = (C_HEADS * HEAD_DIM, C_HEADS * HEAD_DIM, C_HEADS * HEAD_DIM, C_HEADS * HEAD_DIM,
              C_HEADS * C_V_DIM, X_HEADS * HEAD_DIM, ODD_MIX)
EVEN_IN = sum(EVEN_SPLITS)
ODD_IN = sum(ODD_SPLITS)

kernel_name = "hybrid_window_axial_diff_encoder"


def _split(t, sizes):
    idx = [int(s) for s in np.cumsum(sizes)[:-1]]
    return jnp.split(t, idx, axis=-1)


def rmsnorm(x, g):
    xf = x.astype(jnp.float32)
    y = xf * lax.rsqrt(jnp.mean(xf * xf, axis=-1, keepdims=True) + EPS)
    return (y * g.astype(jnp.float32)).astype(x.dtype)


def t5_bucket(rel):
    nb = REL_BUCKETS // 2
    max_exact = nb // 2
    ret = jnp.where(rel > 0, nb, 0)
    n = jnp.abs(rel)
    nf = jnp.maximum(n, 1).astype(jnp.float32)
    large = max_exact + (jnp.log(nf / max_exact) / math.log(REL_MAX_DIST / max_exact)
                         * (nb - max_exact)).astype(jnp.int32)
    large = jnp.minimum(large, nb - 1)
    return ret + jnp.where(n < max_exact, n, large)


def rel_bias_lookup(table, rel):
    return jnp.moveaxis(table[t5_bucket(rel)], -1, 0).astype(jnp.float32)


def windowed_gqa_sink(q, k, v, sink, table):
    Bn, S, H, D = q.shape
    KVH = k.shape[2]
    G = H // KVH
    nb = S // BLOCK
    pad = ((0, 0), (WINDOW, WINDOW), (0, 0), (0, 0))
    kp = jnp.pad(k, pad).reshape(Bn, nb + 2, BLOCK, KVH, D)
    vp = jnp.pad(v, pad).reshape(Bn, nb + 2, BLOCK, KVH, D)
    kw = jnp.concatenate([kp[:, :-2], kp[:, 1:-1], kp[:, 2:]], axis=2)
    vw = jnp.concatenate([vp[:, :-2], vp[:, 1:-1], vp[:, 2:]], axis=2)
    qb = q.reshape(Bn, nb, BLOCK, KVH, G, D)
    logits = jnp.einsum("bnqkgd,bnskd->bnkgqs", qb, kw).astype(jnp.float32) * (D ** -0.5)
    a = jnp.arange(BLOCK)[:, None]
    c = jnp.arange(3 * BLOCK)[None, :]
    rel = c - BLOCK - a
    bias = rel_bias_lookup(table, rel).reshape(KVH, G, BLOCK, 3 * BLOCK)
    kpos = jnp.arange(nb)[:, None] * BLOCK - BLOCK + c
    valid = ((jnp.abs(rel) <= WINDOW)[None]
             & (kpos >= 0)[:, None, :] & (kpos < S)[:, None, :])
    logits = jnp.where(valid[None, :, None, None], logits + bias, NEG_INF)
    sink_col = jnp.broadcast_to(sink.astype(jnp.float32).reshape(1, 1, KVH, G, 1, 1),
                                logits.shape[:-1] + (1,))
    p = jax.nn.softmax(jnp.concatenate([logits, sink_col], axis=-1), axis=-1)[..., :-1]
    out = jnp.einsum("bnkgqs,bnskd->bnqkgd", p.astype(v.dtype), vw)
    return out.reshape(Bn, S, H, D)


def axial_rope_tables(S):
    rows = S // GRID_W
    row = jnp.broadcast_to(jnp.arange(rows)[:, None], (rows, GRID_W)).reshape(-1)
    col = jnp.broadcast_to(jnp.arange(GRID_W)[None, :], (rows, GRID_W)).reshape(-1)
    half = HEAD_DIM // 2
    inv = 1.0 / (ROPE_THETA ** (jnp.arange(0, half, 2, dtype=jnp.float32) / half))
    ang = jnp.concatenate([row.astype(jnp.float32)[:, None] * inv,
                           col.astype(jnp.float32)[:, None] * inv], axis=-1)
    return jnp.cos(ang), jnp.sin(ang)


def apply_axial_rope(x, cos, sin):
    Bn, S, H, D = x.shape
    quarter = D // 4
    xr = x.astype(jnp.float32).reshape(Bn, S, H, 2, 2, quarter)
    x1, x2 = xr[..., 0, :], xr[..., 1, :]
    c = cos.reshape(S, 2, quarter)[None, :, None]
    s = sin.reshape(S, 2, quarter)[None, :, None]
    out = jnp.stack([x1 * c - x2 * s, x1 * s + x2 * c], axis=-2)
    return out.reshape(Bn, S, H, D).astype(x.dtype)


def dense_gqa_blocked(q, k, v):
    Bn, S, H, D = q.shape
    KVH = k.shape[2]
    G = H // KVH
    nb = S // BLOCK
    qb = q.reshape(Bn, nb, BLOCK, KVH, G, D).transpose(1, 0, 2, 3, 4, 5)

    def blk(qi):
        s = jnp.einsum("bqkgd,bskd->bkgqs", qi, k).astype(jnp.float32) * (D ** -0.5)
        p = jax.nn.softmax(s, axis=-1)
        return jnp.einsum("bkgqs,bskd->bqkgd", p.astype(v.dtype), v)

    out = lax.map(blk, qb)
    return out.transpose(1, 0, 2, 3, 4, 5).reshape(Bn, S, H, D)


def diff_attention_blocked(q1, q2, k1, k2, v, lam, table):
    Bn, S, H, D = q1.shape
    nb = S // BLOCK
    qb = jnp.stack([q1, q2], 0).reshape(2, Bn, nb, BLOCK, H, D).transpose(2, 0, 1, 3, 4, 5)
    kk = jnp.stack([k1, k2], 0)
    kpos = jnp.arange(S)

    def blk(args):
        n, qi = args
        s = jnp.einsum("ibqhd,ibshd->ibhqs", qi, kk).astype(jnp.float32) * (D ** -0.5)
        rel = kpos[None, :] - (n * BLOCK + jnp.arange(BLOCK))[:, None]
        p = jax.nn.softmax(s + rel_bias_lookup(table, rel)[None, None], axis=-1)
        w = p[0] - lam * p[1]
        return jnp.einsum("bhqs,bshe->bqhe", w.astype(v.dtype), v)

    out = lax.map(blk, (jnp.arange(nb), qb))
    return out.transpose(1, 0, 2, 3, 4).reshape(Bn, S, H, v.shape[-1])


def memory_cross_attention(q, mem_n, w_mem_kv):
    Bn, M, _ = mem_n.shape
    mk, mv = jnp.split(mem_n @ w_mem_kv, 2, axis=-1)
    mk = mk.reshape(Bn, M, X_HEADS, HEAD_DIM)
    mv = mv.reshape(Bn, M, X_HEADS, HEAD_DIM)
    s = jnp.einsum("bshd,bmhd->bhsm", q, mk).astype(jnp.float32) * (HEAD_DIM ** -0.5)
    p = jax.nn.softmax(s, axis=-1)
    return jnp.einsum("bhsm,bmhd->bshd", p.astype(mv.dtype), mv)


def even_layer(x, mem_n, table, norm_g, w_in, sink, q_norm, k_norm, w_mem_kv, w_out):
    Bn, S, _ = x.shape
    h = rmsnorm(x, norm_g)
    aq, ak, av, bq, bk, bv, xq, gate = _split(h @ w_in, EVEN_SPLITS)
    heads = lambda t, n: t.reshape(Bn, S, n, HEAD_DIM)
    y_a = windowed_gqa_sink(heads(aq, A_HEADS), heads(ak, A_KV_HEADS), heads(av, A_KV_HEADS),
                            sink, table)
    cos, sin = axial_rope_tables(S)
    qb = apply_axial_rope(rmsnorm(heads(bq, B_HEADS), q_norm), cos, sin)
    kb = apply_axial_rope(rmsnorm(heads(bk, B_KV_HEADS), k_norm), cos, sin)
    y_b = dense_gqa_blocked(qb, kb, heads(bv, B_KV_HEADS))
    y_x = memory_cross_attention(heads(xq, X_HEADS), mem_n, w_mem_kv)
    y = jnp.concatenate([y_a.reshape(Bn, S, -1), y_b.reshape(Bn, S, -1),
                         y_x.reshape(Bn, S, -1)], axis=-1) * jax.nn.silu(gate)
    return x + y @ w_out


def odd_layer(x, mem_n, table, norm_g, w_in, lq1, lk1, lq2, lk2, subln_g, w_mem_kv, w_out,
              lam_init):
    Bn, S, _ = x.shape
    h = rmsnorm(x, norm_g)
    q1, q2, k1, k2, v, xq, gate = _split(h @ w_in, ODD_SPLITS)
    heads = lambda t, n, d: t.reshape(Bn, S, n, d)
    lam = (jnp.exp(jnp.sum(lq1.astype(jnp.float32) * lk1.astype(jnp.float32)))
           - jnp.exp(jnp.sum(lq2.astype(jnp.float32) * lk2.astype(jnp.float32))) + lam_init)
    y_c = diff_attention_blocked(heads(q1, C_HEADS, HEAD_DIM), heads(q2, C_HEADS, HEAD_DIM),
                                 heads(k1, C_HEADS, HEAD_DIM), heads(k2, C_HEADS, HEAD_DIM),
                                 heads(v, C_HEADS, C_V_DIM), lam, table)
    y_c = rmsnorm(y_c, subln_g) * (1.0 - lam_init)
    y_x = memory_cross_attention(heads(xq, X_HEADS, HEAD_DIM), mem_n, w_mem_kv)
    y = jnp.concatenate([y_c.reshape(Bn, S, -1), y_x.reshape(Bn, S, -1)], axis=-1) \
        * jax.nn.silu(gate)
    return x + y @ w_out


def setup_inputs(seed: int = 0) -> dict:
    key = jax.random.key(seed)
    ks = jax.random.split(key, 21)
    f32 = jnp.float32
    nrm = lambda k, shape, s: jax.random.normal(k, shape, f32) * s
    gain = lambda k, shape: 1.0 + 0.02 * jax.random.normal(k, shape, f32)
    dinv = D_MODEL ** -0.5
    return {
        "x": nrm(ks[0], (BATCH, SEQ, D_MODEL), 1.0),
        "mem": nrm(ks[1], (BATCH, MEM_LEN, D_MODEL), 1.0),
        "rel_bias": nrm(ks[2], (REL_BUCKETS, REL_HEADS), 0.5),
        "mem_norm": gain(ks[3], (D_MODEL,)),
        "final_norm": gain(ks[4], (D_MODEL,)),
        "even_norm": gain(ks[5], (N_EVEN, D_MODEL)),
        "even_w_in": nrm(ks[6], (N_EVEN, D_MODEL, EVEN_IN), dinv),
        "even_sink": nrm(ks[7], (N_EVEN, A_HEADS), 1.0),
        "even_q_norm": gain(ks[8], (N_EVEN, HEAD_DIM)),
        "even_k_norm": gain(ks[9], (N_EVEN, HEAD_DIM)),
        "even_w_mem_kv": nrm(ks[10], (N_EVEN, D_MODEL, 2 * X_HEADS * HEAD_DIM), dinv),
        "even_w_out": nrm(ks[11], (N_EVEN, EVEN_MIX, D_MODEL), EVEN_MIX ** -0.5),
        "odd_norm": gain(ks[12], (N_ODD, D_MODEL)),
        "odd_w_in": nrm(ks[13], (N_ODD, D_MODEL, ODD_IN), dinv),
        "odd_lambda_q1": nrm(ks[14], (N_ODD, HEAD_DIM), 0.1),
        "odd_lambda_k1": nrm(ks[15], (N_ODD, HEAD_DIM), 0.1),
        "odd_lambda_q2": nrm(ks[16], (N_ODD, HEAD_DIM), 0.1),
        "odd_lambda_k2": nrm(ks[17], (N_ODD, HEAD_DIM), 0.1),
        "odd_subln": gain(ks[18], (N_ODD, C_V_DIM)),
        "odd_w_mem_kv": nrm(ks[19], (N_ODD, D_MODEL, 2 * X_HEADS * HEAD_DIM), dinv),
        "odd_w_out": nrm(ks[20], (N_ODD, ODD_MIX, D_MODEL), ODD_MIX ** -0.5),
    }


def reference(x, mem, rel_bias, mem_norm, final_norm, even_norm, even_w_in, even_sink,
              even_q_norm, even_k_norm, even_w_mem_kv, even_w_out, odd_norm, odd_w_in,
              odd_lambda_q1, odd_lambda_k1, odd_lambda_q2, odd_lambda_k2, odd_subln,
              odd_w_mem_kv, odd_w_out):
    mem_n = rmsnorm(mem, mem_norm)
    h = x
    for i in range(DEPTH):
        j = i // 2
        if i % 2 == 0:
            h = even_layer(h, mem_n, rel_bias, even_norm[j], even_w_in[j], even_sink[j],
                           even_q_norm[j], even_k_norm[j], even_w_mem_kv[j], even_w_out[j])
        else:
            lam_init = 0.8 - 0.6 * math.exp(-0.3 * i)
            h = odd_layer(h, mem_n, rel_bias, odd_norm[j], odd_w_in[j], odd_lambda_q1[j],
                          odd_lambda_k1[j], odd_lambda_q2[j], odd_lambda_k2[j], odd_subln[j],
                          odd_w_mem_kv[j], odd_w_out[j], lam_init)
    return rmsnorm(h, final_norm)
```

```python
import math
import numpy as np
import concourse.bass as bass
import concourse.mybir as mybir
from concourse.bass_utils import run_bass_kernel_spmd

F32 = mybir.dt.float32
BF16 = mybir.dt.bfloat16
AF = mybir.ActivationFunctionType
ALU = mybir.AluOpType
AX = mybir.AxisListType

S = 2048
D = 1024
NT = 16
NC = 8
MEM = 256
EPS = 1e-6
RB = 1280
UW = 1152
LAM_INIT = 0.8 - 0.6 * math.exp(-0.3 * 1)


class Buf:
    __slots__ = ("name", "last_w", "readers", "grp")

    def __init__(self, name, grp=None):
        self.name = name
        self.last_w = None
        self.readers = []
        self.grp = grp if grp is not None else DGroup(name)


class DGroup:
    __slots__ = ("name", "sem", "count", "final")

    def __init__(self, name, final=False):
        self.name = name
        self.sem = None
        self.count = 0
        self.final = final


class Op:
    __slots__ = ("eng", "fn", "deps", "is_dma", "grp", "needs_inc", "val", "wdeps", "iwait")

    def __init__(self, eng, fn, is_dma=False, grp=None):
        self.eng = eng
        self.fn = fn
        self.deps = []
        self.wdeps = []
        self.iwait = False
        self.is_dma = is_dma
        self.grp = grp
        self.needs_inc = False
        self.val = None


class Prog:
    ENGS = ("pe", "act", "dve", "pool", "sp")

    def __init__(self, nc):
        self.nc = nc
        self.ops = []
        self.h = {"pe": nc.tensor, "act": nc.scalar, "dve": nc.vector, "pool": nc.gpsimd, "sp": nc.sync}
        self.out_groups = []

    def _add(self, op, reads, writes, wreads=()):
        deps = []
        for b in wreads:
            if b.last_w is not None and b.last_w is not op:
                op.wdeps.append(b.last_w)
        reads = list(reads) + list(wreads)
        for b in reads:
            if b.last_w is not None:
                deps.append(b.last_w)
        for b in writes:
            if b.last_w is not None:
                deps.append(b.last_w)
            deps.extend(b.readers)
        seen = set()
        for d in deps:
            if id(d) in seen or d is op:
                continue
            seen.add(id(d))
            if d.eng == "pe" and op.eng == "pe" and not d.is_dma and not op.is_dma:
                continue
            op.deps.append(d)
        for b in reads:
            if not op.is_dma:
                b.readers = [r for r in b.readers if r.is_dma or r.eng != op.eng]
            b.readers.append(op)
        for b in writes:
            b.last_w = op
            b.readers = []
        self.ops.append(op)
        return op

    def op(self, eng, fn, reads=(), writes=(), wreads=None):
        o = Op(eng, fn)
        if wreads is not None and eng == "pe":
            o.iwait = True
            return self._add(o, list(reads), list(writes), list(wreads))
        return self._add(o, list(reads), list(writes))

    def dma(self, eng, out_ap, in_ap, reads=(), writes=(), grp=None):
        writes = list(writes)
        if grp is None:
            grp = writes[0].grp
        fn = lambda e: e.dma_start(out=out_ap, in_=in_ap)
        return self._add(Op(eng, fn, is_dma=True, grp=grp), list(reads), writes)

    def emit(self):
        nc = self.nc
        for op in self.ops:
            for d in op.deps:
                d.needs_inc = True
            for d in op.wdeps:
                d.needs_inc = True
        esem = {e: nc.alloc_semaphore("sem_" + e) for e in self.ENGS}
        cnt = {e: 0 for e in self.ENGS}
        for op in self.ops:
            if op.is_dma:
                g = op.grp
                if g.sem is None:
                    g.sem = nc.alloc_semaphore("dsem_" + g.name)
                g.count += 16
                op.val = (g, g.count)
            elif op.needs_inc:
                cnt[op.eng] += 1
                op.val = (esem[op.eng], cnt[op.eng])
        waited = {e: {} for e in self.ENGS}
        nwaits = 0
        for op in self.ops:
            e = self.h[op.eng]
            w = waited[op.eng]
            need = {}
            wkeys = set()
            for d in op.wdeps:
                wkeys.add((d.val[0].sem if d.is_dma else d.val[0]).num)
            for d in op.deps:
                if d.is_dma:
                    g, val = d.val
                    sem = g.sem
                    if g.final:
                        val = g.count
                else:
                    sem, val = d.val
                k = sem.num
                if w.get(k, 0) >= val:
                    continue
                if k not in need or need[k][1] < val:
                    need[k] = (sem, val)
            items = list(need.items())
            attach = None
            if op.iwait and items:
                cand = [it for it in items if it[0] not in wkeys]
                if cand:
                    attach = cand[-1]
                    items = [it for it in items if it is not attach]
            for k, (sem, val) in items:
                e.wait_ge(sem, val)
                w[k] = val
                nwaits += 1
            ins = op.fn(e)
            if attach is not None:
                k, (sem, val) = attach
                ins._wait_ge(sem, val)
                w[k] = val
                nwaits += 1
            if op.is_dma:
                ins.then_inc(op.val[0].sem, 16)
            elif op.needs_inc:
                ins.then_inc(op.val[0], 1)
        seen_g = set()
        for op in self.ops:
            if op.is_dma and id(op.grp) not in seen_g:
                seen_g.add(id(op.grp))
                self.h["sp"].wait_ge(op.grp.sem, op.grp.count)
        return dict(nops=len(self.ops), nwaits=nwaits, cnt=cnt)


def _t5_bucket_np(rel):
    nb, max_exact = 16, 8
    ret = np.where(rel > 0, nb, 0)
    n = np.abs(rel)
    nf = np.maximum(n, 1).astype(np.float32)
    large = max_exact + (np.log(nf / np.float32(max_exact)) / np.float32(math.log(128 / max_exact))
                         * np.float32(nb - max_exact)).astype(np.int32)
    large = np.minimum(large, nb - 1)
    return ret + np.where(n < max_exact, n, large)


MYORDER0 = np.concatenate(
    [np.concatenate([np.arange(64 * j, 64 * j + 64), np.arange(64 * (j + 3), 64 * (j + 3) + 64)]) for j in range(3)]
    + [384 + np.concatenate([np.arange(64 * j, 64 * j + 64), np.arange(64 * (j + 3), 64 * (j + 3) + 64)]) for j in range(3)]
    + [np.arange(768, 1024)])

G0 = dict(gate0=(0, 512), gate1=(512, 512), aqk=(1024, 512), av=(1536, 128), bqk=(1664, 512), bv=(2176, 128),
          xq=(2304, 256))
G1 = dict(gate0=(0, 512), gate1=(512, 512), xq=(1024 + 6 * 384, 256))
for _h in range(6):
    G1["c%d" % _h] = (1024 + 384 * _h, 384)


def _cols0():
    pair = lambda base: np.concatenate(
        [np.concatenate([base + np.arange(64 * j, 64 * j + 64), base + np.arange(64 * (j + 3), 64 * (j + 3) + 64)])
         for j in range(3)])
    return np.concatenate([1536 + MYORDER0, pair(0), np.arange(384, 512), np.arange(512, 640),
                           pair(640), np.arange(1024, 1152), np.arange(1152, 1280), np.arange(1280, 1536)])


def _cols1():
    cols = [2560 + np.arange(1024)]
    for h in range(6):
        cols += [64 * h + np.arange(64), 384 + 64 * h + np.arange(64), 768 + 64 * h + np.arange(64),
                 1152 + 64 * h + np.arange(64), 1536 + 128 * h + np.arange(128)]
    cols.append(np.arange(2304, 2560))
    return np.concatenate(cols)


def _pc(w):
    k, n = w.shape
    return np.ascontiguousarray(w.reshape(k // 128, 128, n).transpose(1, 0, 2))


def _host_consts():
    i = np.arange(RB)
    b = _t5_bucket_np(639 - i)
    oht = np.zeros((32, RB), np.float32)
    oht[b, i] = 1.0
    rows = S // 64
    row = np.repeat(np.arange(rows), 64).astype(np.float64)
    col = np.tile(np.arange(64), rows).astype(np.float64)
    inv = 1.0 / (10000.0 ** (np.arange(0, 32, 2, dtype=np.float32) / np.float32(32))).astype(np.float32)
    ar = (row[:, None].astype(np.float32) * inv).astype(np.float32).astype(np.float64)
    ac = (col[:, None].astype(np.float32) * inv).astype(np.float32).astype(np.float64)
    c0, s0, c1, s1 = np.cos(ar), np.sin(ar), np.cos(ac), np.sin(ac)
    rope = np.concatenate([c0, c0, c1, c1, -s0, s0, -s1, s1], axis=1).astype(np.float32)
    return oht, rope


def build_program(n_layers=2, stop_after=None):
    nc = bass.Bass("TRN2", target_bir_lowering=False)
    P = Prog(nc)
    din = lambda name, shape: nc.dram_tensor(name, list(shape), F32, kind="ExternalInput").ap()
    x_d = din("x", [S, D])
    mem_d = din("mem", [MEM, D])
    tab_d = din("tab", [32, 6])
    oht_d = din("oht", [32, RB])
    rope_d = din("rope", [S, 128])
    gains_d = din("gains", [4, D])
    win_d = [din("w_in0", [128, NC, 2560]), din("w_in1", [128, NC, 3584])]
    wmem_d = [din("w_mem0", [128, NC, 512]), din("w_mem1", [128, NC, 512])]
    wout_d = [din("w_out0", [128, NC, D]), din("w_out1", [128, NC, D])]
    smallc_d = din("smallc", [1, 512])
    qkn_d = din("qkn", [1, 512])
    out_d = nc.dram_tensor("out", [S, D], F32, kind="ExternalOutput").ap()
    scr = nc.dram_tensor("scr", [6, RB], F32, kind="Internal")

    sb = lambda name, shape, dt: nc.alloc_sbuf_tensor(name, list(shape), dt).ap()
    xres = sb("xres", [128, NT, D], F32)
    hT_flat = sb("hT", [128, NC * S], BF16)
    hT = hT_flat.rearrange("p (c t) -> p c t", c=NC)
    hT_f32 = hT_flat.bitcast(F32)
    sgy = sb("sgy", [128, NC, S], BF16)
    qk4_flat = sb("qk4", [128, 5 * S], BF16)
    qk4 = qk4_flat.rearrange("p (i t) -> p i t", i=5)
    qk4_f32 = qk4_flat.bitcast(F32)
    vbuf = sb("vbuf", [128, NT, 256], BF16)
    mkT = [sb("mkT%d" % l, [128, 4, MEM], BF16) for l in range(2)]
    vx = [sb("vx%d" % l, [128, 2, 2, 192], BF16) for l in range(2)]
    wbuf = [sb("wbuf%d" % i, [128, NC, 512], BF16) for i in range(2)]
    U = sb("U", [128, UW], F32)
    biasA = sb("biasA", [128, 384], F32)
    maskA = sb("maskA", [128, 384], F32)
    pt = [sb("pt%d" % i, [128, 512], BF16) for i in range(4)]
    ft = sb("ft", [128, 4, 512], F32)
    hbf = [sb("hbf%d" % i, [128, D], BF16) for i in range(2)]
    ropes = [sb("rope%d" % i, [128, 128], F32) for i in range(2)]
    ident = sb("ident", [128, 128], BF16)
    ones_bf = sb("ones_bf", [128, 128], BF16)
    ones_f = sb("ones_f", [128, 128], F32)
    Jm = sb("Jm", [128, 128], F32)
    tab_s = sb("tab_s", [32, 6], F32)
    epsb = sb("epsb", [128, 1], F32)
    ss = sb("ss", [128, 8], F32)
    ssn = sb("ssn", [128, 4], F32)
    ss8b = sb("ss8b", [128, 8], F32)
    smallc = sb("smallc_s", [128, 512], F32)
    esink = smallc[:, 0:8]
    lamt = smallc[:, 64:320]
    lams = sb("lams", [128, 4], F32)
    gsub = sb("gsub", [128, 1], F32)
    psb = [nc.alloc_psum_tensor("ps%d" % i, [128, 512], F32).ap() for i in range(8)]

    B = {}

    def mk(name, grp=None):
        B[name] = Buf(name, grp)
        return B[name]

    b_x = [mk("x%d" % t) for t in range(NT)]
    b_hTc = [mk("hT%d" % i) for i in range(4)]
    b_sgy = [mk("sgy%d" % c) for c in range(NC)]
    b_qk = [mk("qk%d" % i) for i in range(5)]
    b_v = [mk("v0"), mk("v1")]
    b_mkT = [mk("mkT0"), mk("mkT1")]
    b_vx = [mk("vx0"), mk("vx1")]
    b_w = [mk("w0"), mk("w1")]
    b_U = mk("U")
    b_bA = mk("biasA")
    b_mA = mk("maskA")
    b_pt = [mk("pt%d" % i) for i in range(4)]
    b_ft = [mk("ft%d" % i) for i in range(4)]
    b_hbf = [mk("hbf0"), mk("hbf1")]
    b_rope = [mk("rope0"), mk("rope1")]
    b_c = mk("consts")
    b_ss = mk("ss")
    b_ssn = [mk("ssn0"), mk("ssn1")]
    b_lam = mk("lam")
    b_ps = [mk("ps%d" % i) for i in range(8)]
    b_scr = mk("scr")
    b_out = [mk("out%d" % t, b_x[t].grp) for t in range(NT)]

    ft2 = lambda i: ft[:, 2 * i:2 * i + 2, :].rearrange("p a b -> p (a b)")
    b_ft2 = lambda i: [b_ft[2 * i], b_ft[2 * i + 1]]

    def mm(out, lhsT, rhs, start, stop):
        return lambda e: e.matmul(out, lhsT=lhsT, rhs=rhs, start=start, stop=stop)

    def act(out, in_, func, scale=1.0, bias=None, accum_out=None):
        kw = {}
        if bias is not None:
            kw["bias"] = bias
        if accum_out is not None:
            kw["accum_out"] = accum_out
        return lambda e: e.activation(out=out, in_=in_, func=func, scale=scale, **kw)

    def tt(out, in0, in1, op):
        return lambda e: e.tensor_tensor(out=out, in0=in0, in1=in1, op=op)

    def ts(out, in0, s1, op0, s2=None, op1=None):
        if op1 is None:
            return lambda e: e.tensor_scalar(out=out, in0=in0, scalar1=s1, scalar2=None, op0=op0)
        return lambda e: e.tensor_scalar(out=out, in0=in0, scalar1=s1, scalar2=s2, op0=op0, op1=op1)

    def stt(out, in0, scalar, in1, op0, op1, accum_out=None):
        if accum_out is not None:
            return lambda e: e.scalar_tensor_tensor(out=out, in0=in0, scalar=scalar, in1=in1, op0=op0, op1=op1,
                                                    accum_out=accum_out)
        return lambda e: e.scalar_tensor_tensor(out=out, in0=in0, scalar=scalar, in1=in1, op0=op0, op1=op1)

    def cp(out, in_):
        return lambda e: e.tensor_copy(out=out, in_=in_)

    def rcp(out, in_):
        return lambda e: e.reciprocal(out=out, in_=in_)

    def mset(ap, v):
        return lambda e: e.memset(ap, v)

    P.op("pool", mset(ident, 0.0), writes=[b_c])
    P.op("pool", lambda e: e.affine_select(out=ident, in_=ident, compare_op=ALU.not_equal, fill=1.0, base=0,
                                           pattern=[[-1, 128]], channel_multiplier=1), reads=[b_c], writes=[b_c])
    P.op("pool", mset(Jm, 0.0), writes=[b_c])
    P.op("pool", lambda e: e.affine_select(out=Jm, in_=Jm, compare_op=ALU.not_equal, fill=1.0, base=-127,
                                           pattern=[[1, 128]], channel_multiplier=1), reads=[b_c], writes=[b_c])
    P.op("pool", mset(ones_bf, 1.0), writes=[b_c])
    P.op("pool", mset(ones_f, 1.0), writes=[b_c])
    P.op("pool", mset(epsb, EPS), writes=[b_c])
    P.op("pool", mset(maskA, 0.0), writes=[b_mA])
    P.op("pool", lambda e: e.affine_select(out=maskA[:, 0:128], in_=maskA[:, 0:128], compare_op=ALU.is_ge, fill=-30000.0,
                                           base=0, pattern=[[-1, 128]], channel_multiplier=1), reads=[b_mA], writes=[b_mA])
    P.op("pool", lambda e: e.affine_select(out=maskA[:, 256:384], in_=maskA[:, 256:384], compare_op=ALU.is_ge, fill=-30000.0,
                                           base=0, pattern=[[1, 128]], channel_multiplier=-1), reads=[b_mA], writes=[b_mA])
    P.op("pool", mset(vbuf[:, :, 64:128], 1.0), writes=[b_v[0], b_v[1]])
    for l in range(2):
        P.op("pool", mset(vx[l][:, :, :, 64:128], 1.0), writes=[b_vx[l]])
        P.op("pool", mset(mkT[l], 0.0), writes=[b_mkT[l]])

    P.dma("sp", tab_s, tab_d, writes=[b_c])
    P.dma("sp", smallc, smallc_d.partition_broadcast(128), writes=[b_c, b_lam], grp=b_lam.grp)

    P.op("act", act(esink[:, 0:6], esink[:, 0:6], AF.Exp), reads=[b_c], writes=[b_c])
    jk = hbf[0][:, 0:64]
    P.op("dve", stt(jk, lamt[:, 0:64], 1.0, lamt[:, 64:128], ALU.mult, ALU.mult, accum_out=lams[:, 0:1]),
         reads=[b_lam], writes=[b_hbf[0], b_lam])
    P.op("dve", stt(jk, lamt[:, 128:192], 1.0, lamt[:, 192:256], ALU.mult, ALU.mult, accum_out=lams[:, 1:2]),
         reads=[b_lam], writes=[b_hbf[0], b_lam])
    P.op("act", act(lams[:, 0:2], lams[:, 0:2], AF.Exp), reads=[b_lam], writes=[b_lam])
    P.op("dve", tt(lams[:, 2:3], lams[:, 1:2], lams[:, 0:1], ALU.subtract), reads=[b_lam], writes=[b_lam])
    P.op("dve", ts(lams[:, 2:3], lams[:, 2:3], -LAM_INIT, ALU.add), reads=[b_lam], writes=[b_lam])
    P.op("dve", stt(hbf[0][:, 0:128], smallc[:, 320:448], 1.0 - LAM_INIT, Jm, ALU.mult, ALU.mult, accum_out=gsub),
         reads=[b_lam, b_c], writes=[b_hbf[0], b_lam])
    neg_lam = lams[:, 2:3]

    P.dma("sp", ft[:, 2:4, :].rearrange("p a b -> p (a b)"), gains_d[1:2, :].partition_broadcast(128),
          writes=[b_ft[2], b_ft[3]], grp=b_ft[2].grp)
    for t in range(NT):
        P.dma("sp", xres[:, t, :], x_d[t * 128:(t + 1) * 128, :], writes=[b_x[t]])

    def build_bias_vector():
        oht_s = qk4_f32[0:32, 0:RB]
        fv = qk4_f32[0:6, 2048:2048 + RB]
        bq = b_qk[0:4]
        P.dma("sp", oht_s, oht_d, writes=bq, grp=b_qk[0].grp)
        for j, (c0, n) in enumerate([(0, 512), (512, 512), (1024, 256)]):
            P.op("pe", mm(psb[j][0:6, 0:n], tab_s, oht_s[:, c0:c0 + n], True, True), reads=[b_c] + bq, writes=[b_ps[j]])
            P.op("dve", cp(fv[:, c0:c0 + n], psb[j][0:6, 0:n]), reads=[b_ps[j]], writes=bq)
        P.dma("sp", scr.ap(), fv, reads=bq, writes=[b_scr])

    def build_U_dma(h):
        Hk = ft[:, 0:3, :].rearrange("p a b -> p (a b)")[:, 0:UW]
        P.dma("sp", Hk, bass.AP(scr, h * RB, [[1, 128], [1, UW]]), reads=[b_scr], writes=[b_ft[0], b_ft[1], b_ft[2]],
              grp=b_ft[0].grp)

    def build_U_flip():
        Hk = ft[:, 0:3, :].rearrange("p a b -> p (a b)")[:, 0:UW]
        for j, (c0, n) in enumerate([(0, 512), (512, 512), (1024, 128)]):
            bk = [7, 0, 1][j]
            P.op("pe", mm(psb[bk][:, 0:n], Jm, Hk[:, c0:c0 + n], True, True),
                 reads=[b_c, b_ft[0], b_ft[1], b_ft[2]], writes=[b_ps[bk]])
            P.op("dve", cp(U[:, c0:c0 + n], psb[bk][:, 0:n]), reads=[b_ps[bk]], writes=[b_U])

    biasbuf = [(biasA, b_bA), (U[:, 0:384], b_U)]
    b_Uhk = mk("Uhk")

    def bias_A_dma(h):
        Hk = U[:, 384:768]
        P.dma("sp", Hk, bass.AP(scr, h * RB + 384, [[1, 128], [1, 384]]), reads=[b_scr], writes=[b_Uhk])

    def bias_A_flip(h):
        Hk = U[:, 384:768]
        bb, b_bb = biasbuf[h % 2]
        P.op("pe", mm(psb[6][:, 0:384], Jm, Hk, True, True), reads=[b_c, b_Uhk], writes=[b_ps[6]])
        for jj in range(3):
            P.op("dve", tt(bb[:, jj * 128:(jj + 1) * 128], psb[6][:, 256 - 128 * jj:384 - 128 * jj],
                           maskA[:, jj * 128:(jj + 1) * 128], ALU.add), reads=[b_ps[6], b_mA], writes=[b_bb])

    wstate = {"slot": 0}

    def load_group(src3d, c0, n):
        s_ = wstate["slot"]
        wstate["slot"] ^= 1
        P.dma("pool", wbuf[s_][:, :, 0:n], src3d[:, :, c0:c0 + n], writes=[b_w[s_]])
        return s_

    psrot = {"i": 0}

    def next_bank():
        bk = [7, 0, 1, 2][psrot["i"] % 4]
        psrot["i"] += 1
        return bk

    def norm_a(src, b_src, gslot, k):
        hb = hbf[k % 2]
        bhb = b_hbf[k % 2]
        g_ap = ft2(gslot)
        sc = ssn[:, k % 2:k % 2 + 1]
        bsc = b_ssn[k % 2]
        P.op("act", act(hb, src, AF.Square, scale=float(D ** -0.5), accum_out=sc),
             reads=b_src, writes=[bhb, bsc])
        P.op("act", act(sc, sc, AF.Ln, bias=epsb), reads=[bsc, b_c], writes=[bsc])
        P.op("act", act(sc, sc, AF.Exp, scale=-0.5), reads=[bsc], writes=[bsc])
        P.op("dve", stt(hb, src, sc, g_ap, ALU.mult, ALU.mult),
             reads=b_src + [bsc] + b_ft2(gslot), writes=[bhb])

    def norm_b(dstT, b_dst, tcol, k):
        hb = hbf[k % 2]
        bhb = b_hbf[k % 2]
        bk = next_bank()
        pv = psb[bk].bitcast(BF16)
        for c in range(NC):
            P.op("pe", lambda e, c=c, pv=pv, hb=hb: e.transpose(pv[:, c * 128:(c + 1) * 128], hb[:, c * 128:(c + 1) * 128], ident),
                 reads=[bhb, b_c], writes=[b_ps[bk]])
        P.op("act", act(dstT[:, :, tcol:tcol + 128], pv.rearrange("p (c t) -> p c t", c=NC), AF.Copy),
             reads=[b_ps[bk]], writes=b_dst)

    def norm_transpose(src, b_src, gslot, dstT, b_dst, tcol, k):
        norm_a(src, b_src, gslot, k)
        norm_b(dstT, b_dst, tcol, k)

    def load_gain(idx, gslot):
        P.dma("sp", ft2(gslot), gains_d[idx:idx + 1, :].partition_broadcast(128), writes=b_ft2(gslot),
              grp=b_ft[2 * gslot].grp)

    memT = qk4[:, 0, :].rearrange("p (c t) -> p c t", c=NC)

    def mem_path(wm_slots, after_l):
        load_gain(0, 1)
        for m in range(2):
            P.dma("sp", ft2(0), mem_d[m * 128:(m + 1) * 128, :], writes=b_ft2(0), grp=b_ft[0].grp)
            norm_transpose(ft2(0), b_ft2(0), 1, memT, [b_qk[0]], m * 128, m)
        for l in range(2):
            w = wbuf[wm_slots[l]]
            bw = b_w[wm_slots[l]]
            for ch in range(2):
                bk = next_bank()
                for c in range(NC):
                    P.op("pe", mm(psb[bk][:, 0:MEM], w[:, c, ch * 128:(ch + 1) * 128], memT[:, c, 0:MEM], c == 0, c == NC - 1),
                         reads=[bw, b_qk[0]], writes=[b_ps[bk]])
                P.op("dve", cp(mkT[l][0:64, 2 * ch, :], psb[bk][0:64, 0:MEM]), reads=[b_ps[bk]], writes=[b_mkT[l]])
                P.op("dve", cp(mkT[l][64:128, 2 * ch + 1, :], psb[bk][64:128, 0:MEM]), reads=[b_ps[bk]], writes=[b_mkT[l]])
            for mt in range(2):
                bk = next_bank()
                for c in range(NC):
                    P.op("pe", mm(psb[bk][:, 0:256], memT[:, c, mt * 128:(mt + 1) * 128], w[:, c, 256:512], c == 0, c == NC - 1),
                         reads=[bw, b_qk[0]], writes=[b_ps[bk]])
                pvw = psb[bk][:, 0:256].rearrange("p (pr s d) -> p pr s d", pr=2, s=2)
                P.op("dve", cp(vx[l][:, :, mt, 0:64], pvw[:, :, 0, :]), reads=[b_ps[bk]], writes=[b_vx[l]])
                P.op("dve", cp(vx[l][:, :, mt, 128:192], pvw[:, :, 1, :]), reads=[b_ps[bk]], writes=[b_vx[l]])
            if mt == 1:
                after_l[l]()

    def phase_norm(gain_idx, preloaded=False):
        if not preloaded:
            load_gain(gain_idx, 1)
        for t in range(NT + 1):
            if t < NT:
                norm_a(xres[:, t, :], [b_x[t]], 1, t)
            if t >= 1:
                norm_b(hT, [b_hTc[(t - 1) // 4]], (t - 1) * 128, t - 1)

    def proj_fm(wslot, col0, evac):
        w = wbuf[wslot]
        for tc in range(4):
            bk = next_bank()
            for c in range(NC):
                P.op("pe", mm(psb[bk], w[:, c, col0:col0 + 128], hT[:, c, tc * 512:(tc + 1) * 512], c == 0, c == NC - 1),
                     wreads=[b_w[wslot]], reads=[b_hTc[tc]], writes=[b_ps[bk]])
            evac(bk, tc)

    def gate_phase(wd, groups, after_half=None):
        slots = []
        for gi, g in enumerate(("gate0", "gate1")):
            slots.append(load_group(wd, *groups[g]))
        for gi in range(2):
            if gi == 1 and after_half is not None:
                after_half()
            for cc in range(4):
                ch = gi * 4 + cc

                def evac(bk, tc, ch=ch):
                    P.op("act", act(sgy[:, ch, tc * 512:(tc + 1) * 512], psb[bk], AF.Silu),
                         reads=[b_ps[bk]], writes=[b_sgy[ch]])
                proj_fm(slots[gi], cc * 128, evac)

    st = {"s": 0, "p": 0}

    def run_stream(steps, sbanks=(0, 1, 2)):
        n = len(steps)
        LA = len(sbanks) - 1
        pending = []
        seq = 0
        for i in range(n + LA):
            if i < n:
                sbk = sbanks[st["s"] % len(sbanks)]
                st["s"] += 1
                pti = st["p"] % 4
                st["p"] += 1
                steps[i]["_s"] = sbk
                steps[i]["_p"] = pti
                steps[i]["qk"](sbk)
                steps[i]["sm"](sbk, pti)
            j = i - LA
            if j >= 0:
                steps[j]["pv"](steps[j]["_p"])
                fin = steps[j].get("fin")
                if fin is not None:
                    stages = fin if isinstance(fin, list) else [(1, fin)]
                    for dly, fn in stages:
                        pending.append((i + dly, seq, fn))
                        seq += 1
            due = sorted([p for p in pending if p[0] <= i])
            pending = [p for p in pending if p[0] > i]
            for _, _, fn in due:
                fn()
        for _, _, fn in sorted(pending):
            fn()

    fts = {"i": 0}

    def next_ft():
        i = fts["i"] % 4
        fts["i"] += 1
        return i

    def finalize_half(obank, half, ch, q0, nq, esink_col=None):
        ro = slice(64 * half, 64 * half + 64)
        rd = slice(64 * (1 - half), 64 * (1 - half) + 64)
        ps = psb[obank]
        fa = next_ft()
        fb = next_ft()
        A_ = ft[:, fa, :]
        B_ = ft[:, fb, :]
        if esink_col is not None:
            P.op("act", act(A_[rd, 0:nq], ps[rd, 0:nq], AF.Ln, bias=esink[rd, esink_col:esink_col + 1]),
                 reads=[b_ps[obank], b_c], writes=[b_ft[fa]])
        else:
            P.op("act", act(A_[rd, 0:nq], ps[rd, 0:nq], AF.Ln), reads=[b_ps[obank]], writes=[b_ft[fa]])
        P.op("act", act(B_[ro, 0:nq], A_[rd, 0:nq], AF.Exp, scale=-1.0), reads=[b_ft[fa]], writes=[b_ft[fb]])
        P.op("dve", tt(A_[ro, 0:nq], ps[ro, 0:nq], B_[ro, 0:nq], ALU.mult),
             reads=[b_ps[obank], b_ft[fb]], writes=[b_ft[fa]])
        dst = sgy[ro, ch, q0:q0 + nq]
        P.op("dve", tt(dst, A_[ro, 0:nq], dst, ALU.mult), reads=[b_ft[fa], b_sgy[ch]], writes=[b_sgy[ch]])

    def dense_steps(qT_ap, b_q, kT_ap, b_k, lhsT_v, b_vv, nkb, obank_of, fin_of, kcol=128):
        steps = []
        for qc in range(4):
            for kb in range(nkb):
                ob = obank_of(qc)

                def qk(sbk, qc=qc, kb=kb):
                    P.op("pe", mm(psb[sbk], kT_ap[:, kb * kcol:(kb + 1) * kcol], qT_ap[:, qc * 512:(qc + 1) * 512], True, True),
                         wreads=[b_k], reads=[b_q], writes=[b_ps[sbk]])

                def sm(sbk, pti):
                    P.op("act", act(pt[pti], psb[sbk], AF.Exp, scale=0.125), reads=[b_ps[sbk]], writes=[b_pt[pti]])

                def pv(pti, kb=kb, ob=ob):
                    P.op("pe", mm(psb[ob], lhsT_v(kb), pt[pti], kb == 0, kb == nkb - 1),
                         wreads=[b_vv], reads=[b_pt[pti]], writes=[b_ps[ob]])
                stp = dict(qk=qk, sm=sm, pv=pv)
                if kb == nkb - 1:
                    stp["fin"] = (lambda qc=qc, ob=ob: fin_of(qc, ob))
                steps.append(stp)
        return steps

    def x_attention(l):
        steps = []
        for xh in range(4):
            ch = xh // 2
            half = xh % 2
            steps += dense_steps(
                qk4[:, ch, :], b_qk[ch], mkT[l][:, xh, :], b_mkT[l],
                lambda kb, ch=ch, half=half: vx[l][:, ch, kb, 64 * half:64 * half + 128], b_vx[l], 2,
                lambda qc, xh=xh: 3 + ((xh * 4 + qc) % 4),
                lambda qc, ob, ch=ch, half=half: finalize_half(ob, half, 6 + ch, qc * 512, 512))
        run_stream(steps)

    def out_proj(wd, final, next_gain=None):
        slots = [load_group(wd, 0, 512), load_group(wd, 512, 512)]
        load_gain(3 if final else next_gain, 1)
        for t in range(NT):
            for half in range(2):
                w = wbuf[slots[half]]
                bk = next_bank()
                for c in range(NC):
                    P.op("pe", mm(psb[bk], sgy[:, c, t * 128:(t + 1) * 128], w[:, c, :], c == 0, c == NC - 1),
                         wreads=[b_sgy[c]], reads=[b_w[slots[half]]], writes=[b_ps[bk]])
                xs_ = xres[:, t, half * 512:(half + 1) * 512]
                P.op("dve", tt(xs_, psb[bk], xs_, ALU.add), reads=[b_ps[bk], b_x[t]], writes=[b_x[t]])
            if final:
                src = xres[:, t, :]
                hb = hbf[t % 2]
                sc = ssn[:, t % 2:t % 2 + 1]
                bsc = b_ssn[t % 2]
                P.op("act", act(hb, src, AF.Square, scale=float(D ** -0.5), accum_out=sc),
                     reads=[b_x[t]], writes=[b_hbf[t % 2], bsc])
                P.op("act", act(sc, sc, AF.Ln, bias=epsb), reads=[bsc, b_c], writes=[bsc])
                P.op("act", act(sc, sc, AF.Exp, scale=-0.5), reads=[bsc], writes=[bsc])
                P.op("dve", stt(src, src, sc, ft2(1), ALU.mult, ALU.mult),
                     reads=[b_x[t], bsc] + b_ft2(1), writes=[b_x[t]])
                P.dma("sp", out_d[t * 128:(t + 1) * 128, :], src, reads=[b_x[t]], writes=[b_out[t]])
            else:
                norm_a(xres[:, t, :], [b_x[t]], 1, t)
                if t >= 1:
                    norm_b(hT, [b_hTc[(t - 1) // 4]], (t - 1) * 128, t - 1)
        if not final:
            norm_b(hT, [b_hTc[3]], (NT - 1) * 128, NT - 1)

    class _Stop(Exception):
        pass

    def checkpoint(name):
        if stop_after == name + "dump":
            xflat = xres.rearrange("p t n -> p (t n)")
            for c in range(NC):
                P.op("dve", cp(xflat[:, c * S:(c + 1) * S], sgy[:, c, :]), reads=[b_sgy[c]] + b_x, writes=b_x)
        if stop_after in (name, name + "dump"):
            for t in range(NT):
                P.dma("sp", out_d[t * 128:(t + 1) * 128, :], xres[:, t, :], reads=[b_x[t]], writes=[b_out[t]])
            raise _Stop()

    def body():
        checkpoint("setup")
        phase_norm(1, preloaded=True)
        checkpoint("norm0")
        wsl = {}
        gate_phase(win_d[0], G0, after_half=lambda: wsl.__setitem__("wm0", load_group(wmem_d[0], 0, 512)))
        checkpoint("gate0")
        build_bias_vector()
        wsl["wm1"] = load_group(wmem_d[1], 0, 512)
        mem_path([wsl["wm0"], wsl["wm1"]],
                 [lambda: wsl.__setitem__("aqk", load_group(win_d[0], *G0["aqk"])),
                  lambda: wsl.__setitem__("av", load_group(win_d[0], *G0["av"]))])

        s_aqk = wsl["aqk"]
        s_av = wsl["av"]
        P.op("pool", mset(qk4[64:128, 3, :], 0.0), writes=[b_qk[3]])
        P.op("pool", mset(qk4[0:64, 4, :], 0.0), writes=[b_qk[4]])
        for j in range(4):
            def evac(bk, tc, j=j):
                if j < 3:
                    P.op("dve", cp(qk4[:, j, tc * 512:(tc + 1) * 512], psb[bk]), reads=[b_ps[bk]], writes=[b_qk[j]])
                else:
                    P.op("dve", cp(qk4[0:64, 3, tc * 512:(tc + 1) * 512], psb[bk][0:64, :]), reads=[b_ps[bk]], writes=[b_qk[3]])
                    P.op("dve", cp(qk4[64:128, 4, tc * 512:(tc + 1) * 512], psb[bk][64:128, :]), reads=[b_ps[bk]], writes=[b_qk[4]])
            proj_fm(s_aqk, j * 128, evac)

        def proj_v_tm(wslot, dst_lo, dst_hi):
            w = wbuf[wslot]
            for t in range(NT):
                bk = next_bank()
                for c in range(NC):
                    P.op("pe", mm(psb[bk][:, 0:128], hT[:, c, t * 128:(t + 1) * 128], w[:, c, 0:128], c == 0, c == NC - 1),
                         reads=[b_w[wslot], b_hTc[t // 4]], writes=[b_ps[bk]])
                P.op("dve", cp(vbuf[:, t, 0:64], psb[bk][:, 0:64]), reads=[b_ps[bk]], writes=[b_v[0], b_v[1]])
                P.op("dve", cp(vbuf[:, t, 128:192], psb[bk][:, 64:128]), reads=[b_ps[bk]], writes=[b_v[0], b_v[1]])
        proj_v_tm(s_av, 0, 128)

        def a_head_steps(h):
            j = h % 3
            half = h // 3
            rows = slice(64 * half, 64 * half + 64)
            steps = []
            for qb in range(NT):
                kbs = [kb for kb in (qb - 1, qb, qb + 1) if 0 <= kb < NT]
                c0 = (kbs[0] - qb + 1) * 128
                n = len(kbs) * 128
                ob = 3 + (qb // 4) % 3
                bb, b_bb = biasbuf[h % 2]

                def qk(sbk, qb=qb, kbs=kbs):
                    for kb in kbs:
                        jj = kb - qb + 1
                        P.op("pe", mm(psb[sbk][:, jj * 128:(jj + 1) * 128], qk4[:, 3 + half, kb * 128:(kb + 1) * 128],
                                      qk4[:, j, qb * 128:(qb + 1) * 128], True, True),
                             wreads=[b_qk[3 + half]], reads=[b_qk[j]], writes=[b_ps[sbk]])

                def sm(sbk, pti, c0=c0, n=n):
                    P.op("dve", stt(psb[sbk][:, c0:c0 + n], psb[sbk][:, c0:c0 + n], 0.125, bb[:, c0:c0 + n], ALU.mult, ALU.add),
                         reads=[b_ps[sbk], b_bb], writes=[b_ps[sbk]])
                    P.op("act", act(pt[pti][:, c0:c0 + n], psb[sbk][:, c0:c0 + n], AF.Exp), reads=[b_ps[sbk]], writes=[b_pt[pti]])

                def pv(pti, qb=qb, kbs=kbs, ob=ob):
                    for idx, kb in enumerate(kbs):
                        jj = kb - qb + 1
                        P.op("pe", mm(psb[ob][:, (qb % 4) * 128:(qb % 4) * 128 + 128], vbuf[:, kb, 64 * half:64 * half + 128],
                                      pt[pti][:, jj * 128:(jj + 1) * 128], idx == 0, idx == len(kbs) - 1),
                             wreads=[b_v[0]], reads=[b_pt[pti]], writes=[b_ps[ob]])
                stp = dict(qk=qk, sm=sm, pv=pv)
                if qb % 4 == 3:
                    stp["fin"] = [(1, (lambda qc=qb // 4, ob=ob: finalize_half(ob, half, j, qc * 512, 512, esink_col=h)))]
                if qb == 5 and h + 1 < 6:
                    stp["fin"] = [(0, (lambda: bias_A_flip(h + 1)))]
                steps.append(stp)
            return steps

        checkpoint("aproj")
        bias_A_dma(0)
        bias_A_flip(0)
        for h in range(6):
            if h + 1 < 6:
                bias_A_dma(h + 1)
            run_stream(a_head_steps(h), sbanks=(0, 1, 2, 7))
        checkpoint("A")

        s_bqk = load_group(win_d[0], *G0["bqk"])
        s_bv = load_group(win_d[0], *G0["bv"])
        P.dma("sp", ft[:, 3, :], qkn_d.partition_broadcast(128), writes=[b_ft[3]])
        qkn_t = psb[6]
        P.op("dve", cp(qkn_t, ft[:, 3, :]), reads=[b_ft[3]], writes=[b_ps[6]])
        xs_of = lambda t: (ft[:, 0, :], b_ft[0]) if t % 2 == 0 else (ft[:, 3, :], b_ft[3])
        b_ss2 = [b_ss, b_ssn[0]]
        ss_of = lambda t: (ss[:, 0:8], b_ss) if t % 2 == 0 else (ss8b, b_ssn[0])

        def b_stage1(t):
            rs = ropes[t % 2]
            P.dma("sp", rs, rope_d[t * 128:(t + 1) * 128, :], writes=[b_rope[t % 2]])
            bk = [0, 1][t % 2]
            for c in range(NC):
                P.op("pe", mm(psb[bk], hT[:, c, t * 128:(t + 1) * 128], wbuf[s_bqk][:, c, :], c == 0, c == NC - 1),
                     reads=[b_w[s_bqk], b_hTc[t // 4]], writes=[b_ps[bk]])
            bk2 = [2, 7][t % 2]
            for c in range(NC):
                P.op("pe", mm(psb[bk2][:, 0:128], hT[:, c, t * 128:(t + 1) * 128], wbuf[s_bv][:, c, 0:128], c == 0, c == NC - 1),
                     reads=[b_w[s_bv], b_hTc[t // 4]], writes=[b_ps[bk2]])
            xs_, bxs = xs_of(t)
            P.op("act", act(xs_, psb[bk], AF.Copy), reads=[b_ps[bk]], writes=[bxs])
            P.op("act", act(vbuf[:, t, 0:64], psb[bk2][:, 0:64], AF.Copy), reads=[b_ps[bk2]], writes=[b_v[0], b_v[1]])
            P.op("act", act(vbuf[:, t, 128:192], psb[bk2][:, 64:128], AF.Copy), reads=[b_ps[bk2]], writes=[b_v[0], b_v[1]])

        def b_stage2a(t):
            xs_, bxs = xs_of(t)
            ssv, bssv = ss_of(t)
            sqp = psb[5]
            P.op("dve", tt(sqp, xs_, xs_, ALU.mult), reads=[bxs], writes=[b_ps[5]])
            P.op("dve", lambda e, ssv=ssv: e.tensor_reduce(out=ssv, in_=sqp.rearrange("p (h d) -> p h d", h=8), axis=AX.X, op=ALU.add),
                 reads=[b_ps[5]], writes=[bssv])
            P.op("act", act(ssv, ssv, AF.Ln, scale=1.0 / 64, bias=epsb), reads=[bssv, b_c], writes=[bssv])
            P.op("act", act(ssv, ssv, AF.Exp, scale=-0.5), reads=[bssv], writes=[bssv])

        def b_stage2b(t):
            rs = ropes[t % 2]
            xs_, bxs = xs_of(t)
            ssv, bssv = ss_of(t)
            sq_ = ft[:, 1, :]
            t2_ = ft[:, 2, :]
            x3 = xs_.rearrange("p (h d) -> p h d", h=8)
            P.op("dve", tt(x3, x3, ssv.unsqueeze(2).to_broadcast([128, 8, 64]), ALU.mult),
                 reads=[bxs, bssv], writes=[bxs])
            P.op("dve", tt(xs_, xs_, qkn_t, ALU.mult), reads=[bxs, b_ps[6]], writes=[bxs])
            cc = rs[:, 0:64].unsqueeze(1).to_broadcast([128, 8, 64])
            P.op("dve", tt(sq_.rearrange("p (h d) -> p h d", h=8), x3, cc, ALU.mult),
                 reads=[bxs, b_rope[t % 2]], writes=[b_ft[1]])
            x4 = xs_.rearrange("p (h a s d) -> p h a s d", h=8, a=2, s=2)
            t4 = t2_.rearrange("p (h a s d) -> p h a s d", h=8, a=2, s=2)
            s4 = rs[:, 64:128].rearrange("p (a s d) -> p a s d", a=2, s=2)
            for s_ in range(2):
                P.op("dve", tt(t4[:, :, :, s_, :], x4[:, :, :, 1 - s_, :],
                               s4[:, :, s_, :].unsqueeze(1).to_broadcast([128, 8, 2, 16]), ALU.mult),
                     reads=[bxs, b_rope[t % 2]], writes=[b_ft[2]])
            hb = hbf[t % 2]
            P.op("dve", tt(hb[:, 0:512], sq_, t2_, ALU.add), reads=[b_ft[1], b_ft[2]], writes=[b_hbf[t % 2]])

        def b_stage3(t):
            hb = hbf[t % 2]
            bk3 = [3, 4][t % 2]
            pvv = psb[bk3].bitcast(BF16)
            for c in range(4):
                P.op("pe", lambda e, c=c, pvv=pvv, hb=hb: e.transpose(pvv[:, c * 128:(c + 1) * 128], hb[:, c * 128:(c + 1) * 128], ident),
                     reads=[b_hbf[t % 2], b_c], writes=[b_ps[bk3]])
            P.op("act", act(qk4[:, 0:3, t * 128:(t + 1) * 128], pvv[:, 0:384].rearrange("p (c t) -> p c t", c=3), AF.Copy),
                 reads=[b_ps[bk3]], writes=b_qk[0:3])
            P.op("act", act(qk4[0:64, 3, t * 128:(t + 1) * 128], pvv[0:64, 384:512], AF.Copy), reads=[b_ps[bk3]], writes=[b_qk[3]])
            P.op("act", act(qk4[64:128, 4, t * 128:(t + 1) * 128], pvv[64:128, 384:512], AF.Copy), reads=[b_ps[bk3]], writes=[b_qk[4]])

        b_stage1(0)
        b_stage2a(0)
        for t in range(NT):
            if t + 1 < NT:
                b_stage1(t + 1)
            b_stage2b(t)
            if t + 1 < NT:
                b_stage2a(t + 1)
            b_stage3(t)

        checkpoint("bprep")
        steps = []
        for h in range(6):
            j = h % 3
            half = h // 3
            steps += dense_steps(
                qk4[:, j, :], b_qk[j], qk4[:, 3 + half, :], b_qk[3 + half],
                lambda kb, half=half: vbuf[:, kb, 64 * half:64 * half + 128], b_v[0], NT,
                lambda qc, h=h: 3 + ((h * 4 + qc) % 4),
                lambda qc, ob, j=j, half=half: finalize_half(ob, half, 3 + j, qc * 512, 512))
        run_stream(steps)
        checkpoint("B")

        def x_proj(wd, grp):
            s_x = load_group(wd, *grp)
            for ch in range(2):
                def evac(bk, tc, ch=ch):
                    P.op("dve", cp(qk4[:, ch, tc * 512:(tc + 1) * 512], psb[bk]), reads=[b_ps[bk]], writes=[b_qk[ch]])
                proj_fm(s_x, ch * 128, evac)

        x_proj(win_d[0], G0["xq"])
        x_attention(0)
        checkpoint("X0")
        out_proj(wout_d[0], final=(n_layers == 1), next_gain=2)
        checkpoint("out0")

        if n_layers == 2:
            gate_phase(win_d[1], G1)
            for h in range(6):
                build_U_dma(h)
                s_c = load_group(win_d[1], *G1["c%d" % h])
                qi = 0 if h % 2 == 0 else 3
                if h == 0:
                    P.op("pool", mset(qk4[64:128, 1, :], 0.0), writes=[b_qk[1]])
                    P.op("pool", mset(qk4[0:64, 2, :], 0.0), writes=[b_qk[2]])
                vsl = slice((h % 2) * 128, (h % 2) * 128 + 128)
                bvh = b_v[h % 2]
                def evac_q(bk, tc):
                    P.op("dve", cp(qk4[:, qi, tc * 512:(tc + 1) * 512], psb[bk]), reads=[b_ps[bk]], writes=[b_qk[qi]])

                def evac_k(bk, tc):
                    P.op("dve", cp(qk4[0:64, 1, tc * 512:(tc + 1) * 512], psb[bk][0:64, :]), reads=[b_ps[bk]], writes=[b_qk[1]])
                    P.op("dve", cp(qk4[64:128, 2, tc * 512:(tc + 1) * 512], psb[bk][64:128, :]), reads=[b_ps[bk]], writes=[b_qk[2]])
                proj_fm(s_c, 0, evac_q)
                proj_fm(s_c, 128, evac_k)
                for t in range(NT):
                    bk = next_bank()
                    for c in range(NC):
                        P.op("pe", mm(psb[bk][:, 0:128], hT[:, c, t * 128:(t + 1) * 128], wbuf[s_c][:, c, 256:384], c == 0, c == NC - 1),
                             reads=[b_w[s_c], b_hTc[t // 4]], writes=[b_ps[bk]])
                    P.op("act", act(vbuf[:, t, vsl], psb[bk][:, 0:128], AF.Copy), reads=[b_ps[bk]], writes=[bvh])
                build_U_flip()
                steps = []
                for qc in range(4):
                    for s_ in range(2):
                        rows = slice(64 * s_, 64 * s_ + 64)
                        ob, db = (3, 4) if s_ == 0 else (5, 6)
                        for kb in range(NT):
                            delta = kb - 4 * qc

                            def qk(sbk, kb=kb, qc=qc, s_=s_):
                                P.op("pe", mm(psb[sbk], qk4[:, 1 + s_, kb * 128:(kb + 1) * 128], qk4[:, qi, qc * 512:(qc + 1) * 512], True, True),
                                     wreads=[b_qk[1 + s_]], reads=[b_qk[qi]], writes=[b_ps[sbk]])

                            def sm(sbk, pti, delta=delta):
                                if -1 <= delta <= 4:
                                    m0 = 512 - 128 * delta
                                    P.op("dve", stt(psb[sbk], psb[sbk], 0.125, U[:, m0:m0 + 512], ALU.mult, ALU.add),
                                         reads=[b_ps[sbk], b_U], writes=[b_ps[sbk]])
                                    P.op("act", act(pt[pti], psb[sbk], AF.Exp), reads=[b_ps[sbk]], writes=[b_pt[pti]])
                                else:
                                    col = UW - 1 if delta <= -2 else 0
                                    P.op("act", act(pt[pti], psb[sbk], AF.Exp, scale=0.125, bias=U[:, col:col + 1]),
                                         reads=[b_ps[sbk], b_U], writes=[b_pt[pti]])

                            def pv(pti, kb=kb, ob=ob, db=db):
                                P.op("pe", mm(psb[ob], vbuf[:, kb, vsl], pt[pti], kb == 0, kb == NT - 1),
                                     wreads=[bvh], reads=[b_pt[pti]], writes=[b_ps[ob]])
                                P.op("pe", mm(psb[db], ones_bf, pt[pti], kb == 0, kb == NT - 1),
                                     wreads=[b_c], reads=[b_pt[pti]], writes=[b_ps[db]])
                            stp = dict(qk=qk, sm=sm, pv=pv)
                            if kb == NT - 1:
                                if s_ == 0:
                                    def fin(ob=ob, db=db):
                                        P.op("act", act(ft[:, 0, :], psb[db], AF.Ln), reads=[b_ps[db]], writes=[b_ft[0]])
                                        P.op("act", act(ft[:, 0, :], ft[:, 0, :], AF.Exp, scale=-1.0), reads=[b_ft[0]], writes=[b_ft[0]])
                                        P.op("dve", tt(ft[:, 1, :], psb[ob], ft[:, 0, :], ALU.mult),
                                             reads=[b_ps[ob], b_ft[0]], writes=[b_ft[1]])
                                else:
                                    r2, t1, d_, q_ = ft[:, 0, :], ft[:, 1, :], ft[:, 2, :], ft[:, 3, :]

                                    def fin(ob=ob, db=db):
                                        P.op("act", act(r2, psb[db], AF.Ln), reads=[b_ps[db]], writes=[b_ft[0]])
                                        P.op("act", act(r2, r2, AF.Exp, scale=-1.0), reads=[b_ft[0]], writes=[b_ft[0]])
                                        P.op("dve", tt(d_, psb[ob], r2, ALU.mult), reads=[b_ps[ob], b_ft[0]], writes=[b_ft[2]])
                                        P.op("dve", stt(d_, d_, neg_lam, t1, ALU.mult, ALU.add),
                                             reads=[b_ft[2], b_ft[1], b_lam], writes=[b_ft[2]])
                                        P.op("dve", tt(q_, d_, d_, ALU.mult), reads=[b_ft[2]], writes=[b_ft[3]])

                                    def finB(db=db):
                                        P.op("pe", mm(psb[db], ones_f, q_, True, True), reads=[b_c, b_ft[3]], writes=[b_ps[db]])
                                        P.op("act", act(q_, psb[db], AF.Ln, scale=1.0 / 128, bias=epsb),
                                             reads=[b_ps[db], b_c], writes=[b_ft[3]])
                                        P.op("act", act(q_, q_, AF.Exp, scale=-0.5), reads=[b_ft[3]], writes=[b_ft[3]])

                                    def finC(qc=qc, h=h):
                                        P.op("dve", tt(d_, d_, q_, ALU.mult), reads=[b_ft[2], b_ft[3]], writes=[b_ft[2]])
                                        dst = sgy[:, h, qc * 512:(qc + 1) * 512]
                                        P.op("dve", stt(dst, d_, gsub, dst, ALU.mult, ALU.mult),
                                             reads=[b_ft[2], b_lam, b_sgy[h]], writes=[b_sgy[h]])
                                stp["fin"] = [(1, fin)] if s_ == 0 else [(1, fin), (9, finB), (12, finC)]
                            steps.append(stp)
                run_stream(steps, sbanks=(0, 1, 2, 7))
            x_proj(win_d[1], G1["xq"])
            x_attention(1)
            out_proj(wout_d[1], final=True)


    try:
        body()
    except _Stop:
        pass
    stats = P.emit()
    return nc, stats


_CACHE = {}


def _prep_weights(inputs):
    f = lambda a: np.asarray(a, dtype=np.float32)
    oht, rope = _host_consts()
    w_in0 = _pc(f(inputs["even_w_in"])[0][:, _cols0()])
    w_in1 = _pc(f(inputs["odd_w_in"])[0][:, _cols1()])
    w_out0 = _pc(f(inputs["even_w_out"])[0][MYORDER0, :])
    w_out1 = _pc(f(inputs["odd_w_out"])[0])
    w_mem0 = _pc(f(inputs["even_w_mem_kv"])[0])
    w_mem1 = _pc(f(inputs["odd_w_mem_kv"])[0])
    gains = np.stack([f(inputs["mem_norm"]), f(inputs["even_norm"])[0], f(inputs["odd_norm"])[0],
                      f(inputs["final_norm"])]).astype(np.float32)
    qn, kn = f(inputs["even_q_norm"])[0], f(inputs["even_k_norm"])[0]
    qkn = np.concatenate([np.tile(qn, 6), np.tile(kn, 2)])[None, :].astype(np.float32)
    lamv = np.concatenate([f(inputs["odd_lambda_q1"])[0], f(inputs["odd_lambda_k1"])[0],
                           f(inputs["odd_lambda_q2"])[0], f(inputs["odd_lambda_k2"])[0]])[None, :].astype(np.float32)
    smallc = np.zeros((1, 512), np.float32)
    smallc[0, 0:6] = f(inputs["even_sink"])[0]
    smallc[0, 64:320] = lamv[0]
    smallc[0, 320:448] = f(inputs["odd_subln"])[0][::-1]
    return dict(tab=np.ascontiguousarray(f(inputs["rel_bias"])), oht=oht, rope=rope, gains=gains,
                w_in0=w_in0, w_in1=w_in1, w_mem0=w_mem0, w_mem1=w_mem1, w_out0=w_out0, w_out1=w_out1,
                smallc=smallc, qkn=qkn)


def kernel(**inputs):
    if "nc" not in _CACHE:
        _CACHE["nc"], _CACHE["stats"] = build_program()
    nc = _CACHE["nc"]
    shared = _prep_weights(inputs)
    x = np.asarray(inputs["x"], dtype=np.float32)
    mem = np.asarray(inputs["mem"], dtype=np.float32)
    in_maps = []
    for b in range(8):
        m = dict(shared)
        m["x"] = np.ascontiguousarray(x[b])
        m["mem"] = np.ascontiguousarray(mem[b])
        in_maps.append(m)
    res = run_bass_kernel_spmd(nc, in_maps, core_ids=list(range(8)))
    return np.stack([np.asarray(r["out"], dtype=np.float32) for r in res.results], axis=0)
```

```python
import math
import numpy as np
import concourse.bass as bass
import concourse.mybir as mybir
from concourse.bass_utils import run_bass_kernel_spmd

F32 = mybir.dt.float32
BF16 = mybir.dt.bfloat16
AF = mybir.ActivationFunctionType
ALU = mybir.AluOpType
AX = mybir.AxisListType

S = 2048
D = 1024
NT = 16
NC = 8
MEM = 256
EPS = 1e-6
RB = 1280
UW = 1152
LAM_INIT = 0.8 - 0.6 * math.exp(-0.3 * 1)


class Buf:
    __slots__ = ("name", "last_w", "readers", "grp")

    def __init__(self, name, grp=None):
        self.name = name
        self.last_w = None
        self.readers = []
        self.grp = grp if grp is not None else DGroup(name)


class DGroup:
    __slots__ = ("name", "sem", "count", "final")

    def __init__(self, name, final=False):
        self.name = name
        self.sem = None
        self.count = 0
        self.final = final


class Op:
    __slots__ = ("eng", "fn", "deps", "is_dma", "grp", "needs_inc", "val", "wdeps", "iwait")

    def __init__(self, eng, fn, is_dma=False, grp=None):
        self.eng = eng
        self.fn = fn
        self.deps = []
        self.wdeps = []
        self.iwait = False
        self.is_dma = is_dma
        self.grp = grp
        self.needs_inc = False
        self.val = None


class Prog:
    ENGS = ("pe", "act", "dve", "pool", "sp")

    def __init__(self, nc):
        self.nc = nc
        self.ops = []
        self.h = {"pe": nc.tensor, "act": nc.scalar, "dve": nc.vector, "pool": nc.gpsimd, "sp": nc.sync}
        self.out_groups = []

    def _add(self, op, reads, writes, wreads=()):
        deps = []
        for b in wreads:
            if b.last_w is not None and b.last_w is not op:
                op.wdeps.append(b.last_w)
        reads = list(reads) + list(wreads)
        for b in reads:
            if b.last_w is not None:
                deps.append(b.last_w)
        for b in writes:
            if b.last_w is not None:
                deps.append(b.last_w)
            deps.extend(b.readers)
        seen = set()
        for d in deps:
            if id(d) in seen or d is op:
                continue
            seen.add(id(d))
            if d.eng == "pe" and op.eng == "pe" and not d.is_dma and not op.is_dma:
                continue
            op.deps.append(d)
        for b in reads:
            if not op.is_dma:
                b.readers = [r for r in b.readers if r.is_dma or r.eng != op.eng]
            b.readers.append(op)
        for b in writes:
            b.last_w = op
            b.readers = []
        self.ops.append(op)
        return op

    def op(self, eng, fn, reads=(), writes=(), wreads=None):
        o = Op(eng, fn)
        if wreads is not None and eng == "pe":
            o.iwait = True
            return self._add(o, list(reads), list(writes), list(wreads))
        return self._add(o, list(reads), list(writes))

    def dma(self, eng, out_ap, in_ap, reads=(), writes=(), grp=None):
        writes = list(writes)
        if grp is None:
            grp = writes[0].grp
        fn = lambda e: e.dma_start(out=out_ap, in_=in_ap)
        return self._add(Op(eng, fn, is_dma=True, grp=grp), list(reads), writes)

    def emit(self):
        nc = self.nc
        for op in self.ops:
            for d in op.deps:
                d.needs_inc = True
            for d in op.wdeps:
                d.needs_inc = True
        esem = {e: nc.alloc_semaphore("sem_" + e) for e in self.ENGS}
        cnt = {e: 0 for e in self.ENGS}
        for op in self.ops:
            if op.is_dma:
                g = op.grp
                if g.sem is None:
                    g.sem = nc.alloc_semaphore("dsem_" + g.name)
                g.count += 16
                op.val = (g, g.count)
            elif op.needs_inc:
                cnt[op.eng] += 1
                op.val = (esem[op.eng], cnt[op.eng])
        waited = {e: {} for e in self.ENGS}
        nwaits = 0
        for op in self.ops:
            e = self.h[op.eng]
            w = waited[op.eng]
            need = {}
            wkeys = set()
            for d in op.wdeps:
                wkeys.add((d.val[0].sem if d.is_dma else d.val[0]).num)
            for d in op.deps:
                if d.is_dma:
                    g, val = d.val
                    sem = g.sem
                    if g.final:
                        val = g.count
                else:
                    sem, val = d.val
                k = sem.num
                if w.get(k, 0) >= val:
                    continue
                if k not in need or need[k][1] < val:
                    need[k] = (sem, val)
            items = list(need.items())
            attach = None
            if op.iwait and items:
                cand = [it for it in items if it[0] not in wkeys]
                if cand:
                    attach = cand[-1]
                    items = [it for it in items if it is not attach]
            for k, (sem, val) in items:
                e.wait_ge(sem, val)
                w[k] = val
                nwaits += 1
            ins = op.fn(e)
            if attach is not None:
                k, (sem, val) = attach
                ins._wait_ge(sem, val)
                w[k] = val
                nwaits += 1
            if op.is_dma:
                ins.then_inc(op.val[0].sem, 16)
            elif op.needs_inc:
                ins.then_inc(op.val[0], 1)
        seen_g = set()
        for op in self.ops:
            if op.is_dma and id(op.grp) not in seen_g:
                seen_g.add(id(op.grp))
                self.h["sp"].wait_ge(op.grp.sem, op.grp.count)
        return dict(nops=len(self.ops), nwaits=nwaits, cnt=cnt)


def _t5_bucket_np(rel):
    nb, max_exact = 16, 8
    ret = np.where(rel > 0, nb, 0)
    n = np.abs(rel)
    nf = np.maximum(n, 1).astype(np.float32)
    large = max_exact + (np.log(nf / np.float32(max_exact)) / np.float32(math.log(128 / max_exact))
                         * np.float32(nb - max_exact)).astype(np.int32)
    large = np.minimum(large, nb - 1)
    return ret + np.where(n < max_exact, n, large)


MYORDER0 = np.concatenate(
    [np.concatenate([np.arange(64 * j, 64 * j + 64), np.arange(64 * (j + 3), 64 * (j + 3) + 64)]) for j in range(3)]
    + [384 + np.concatenate([np.arange(64 * j, 64 * j + 64), np.arange(64 * (j + 3), 64 * (j + 3) + 64)]) for j in range(3)]
    + [np.arange(768, 1024)])

G0 = dict(gate0=(0, 512), gate1=(512, 512), aqk=(1024, 512), av=(1536, 128), bqk=(1664, 512), bv=(2176, 128),
          xq=(2304, 256))
G1 = dict(gate0=(0, 512), gate1=(512, 512), xq=(1024 + 6 * 384, 256))
for _h in range(6):
    G1["c%d" % _h] = (1024 + 384 * _h, 384)


def _cols0():
    pair = lambda base: np.concatenate(
        [np.concatenate([base + np.arange(64 * j, 64 * j + 64), base + np.arange(64 * (j + 3), 64 * (j + 3) + 64)])
         for j in range(3)])
    return np.concatenate([1536 + MYORDER0, pair(0), np.arange(384, 512), np.arange(512, 640),
                           pair(640), np.arange(1024, 1152), np.arange(1152, 1280), np.arange(1280, 1536)])


def _cols1():
    cols = [2560 + np.arange(1024)]
    for h in range(6):
        cols += [64 * h + np.arange(64), 384 + 64 * h + np.arange(64), 768 + 64 * h + np.arange(64),
                 1152 + 64 * h + np.arange(64), 1536 + 128 * h + np.arange(128)]
    cols.append(np.arange(2304, 2560))
    return np.concatenate(cols)


def _pc(w):
    k, n = w.shape
    return np.ascontiguousarray(w.reshape(k // 128, 128, n).transpose(1, 0, 2))


def _host_consts():
    i = np.arange(RB)
    b = _t5_bucket_np(639 - i)
    oht = np.zeros((32, RB), np.float32)
    oht[b, i] = 1.0
    rows = S // 64
    row = np.repeat(np.arange(rows), 64).astype(np.float64)
    col = np.tile(np.arange(64), rows).astype(np.float64)
    inv = 1.0 / (10000.0 ** (np.arange(0, 32, 2, dtype=np.float32) / np.float32(32))).astype(np.float32)
    ar = (row[:, None].astype(np.float32) * inv).astype(np.float32).astype(np.float64)
    ac = (col[:, None].astype(np.float32) * inv).astype(np.float32).astype(np.float64)
    c0, s0, c1, s1 = np.cos(ar), np.sin(ar), np.cos(ac), np.sin(ac)
    rope = np.concatenate([c0, c0, c1, c1, -s0, s0, -s1, s1], axis=1).astype(np.float32)
    return oht, rope


def build_program(n_layers=2, stop_after=None):
    nc = bass.Bass("TRN2", target_bir_lowering=False)
    P = Prog(nc)
    din = lambda name, shape: nc.dram_tensor(name, list(shape), F32, kind="ExternalInput").ap()
    x_d = din("x", [S, D])
    mem_d = din("mem", [MEM, D])
    tab_d = din("tab", [32, 6])
    oht_d = din("oht", [32, RB])
    rope_d = din("rope", [S, 128])
    gains_d = din("gains", [4, D])
    win_d = [din("w_in0", [128, NC, 2560]), din("w_in1", [128, NC, 3584])]
    wmem_d = [din("w_mem0", [128, NC, 512]), din("w_mem1", [128, NC, 512])]
    wout_d = [din("w_out0", [128, NC, D]), din("w_out1", [128, NC, D])]
    smallc_d = din("smallc", [1, 512])
    qkn_d = din("qkn", [1, 512])
    out_d = nc.dram_tensor("out", [S, D], F32, kind="ExternalOutput").ap()
    scr = nc.dram_tensor("scr", [6, RB], F32, kind="Internal")

    sb = lambda name, shape, dt: nc.alloc_sbuf_tensor(name, list(shape), dt).ap()
    xres = sb("xres", [128, NT, D], F32)
    hT_flat = sb("hT", [128, NC * S], BF16)
    hT = hT_flat.rearrange("p (c t) -> p c t", c=NC)
    hT_f32 = hT_flat.bitcast(F32)
    sgy = sb("sgy", [128, NC, S], BF16)
    qk4_flat = sb("qk4", [128, 5 * S], BF16)
    qk4 = qk4_flat.rearrange("p (i t) -> p i t", i=5)
    qk4_f32 = qk4_flat.bitcast(F32)
    vbuf = sb("vbuf", [128, NT, 256], BF16)
    mkT = [sb("mkT%d" % l, [128, 4, MEM], BF16) for l in range(2)]
    vx = [sb("vx%d" % l, [128, 2, 2, 192], BF16) for l in range(2)]
    wbuf = [sb("wbuf%d" % i, [128, NC, 512], BF16) for i in range(2)]
    U = sb("U", [128, UW], F32)
    biasA = sb("biasA", [128, 384], F32)
    maskA = sb("maskA", [128, 384], F32)
    pt = [sb("pt%d" % i, [128, 512], BF16) for i in range(4)]
    ft = sb("ft", [128, 4, 512], F32)
    hbf = [sb("hbf%d" % i, [128, D], BF16) for i in range(2)]
    ropes = [sb("rope%d" % i, [128, 128], F32) for i in range(2)]
    ident = sb("ident", [128, 128], BF16)
    ones_bf = sb("ones_bf", [128, 128], BF16)
    ones_f = sb("ones_f", [128, 128], F32)
    Jm = sb("Jm", [128, 128], F32)
    tab_s = sb("tab_s", [32, 6], F32)
    epsb = sb("epsb", [128, 1], F32)
    ss = sb("ss", [128, 8], F32)
    ssn = sb("ssn", [128, 4], F32)
    ss8b = sb("ss8b", [128, 8], F32)
    smallc = sb("smallc_s", [128, 512], F32)
    esink = smallc[:, 0:8]
    lamt = smallc[:, 64:320]
    lams = sb("lams", [128, 4], F32)
    gsub = sb("gsub", [128, 1], F32)
    psb = [nc.alloc_psum_tensor("ps%d" % i, [128, 512], F32).ap() for i in range(8)]

    B = {}

    def mk(name, grp=None):
        B[name] = Buf(name, grp)
        return B[name]

    b_x = [mk("x%d" % t) for t in range(NT)]
    b_hTc = [mk("hT%d" % i) for i in range(4)]
    b_sgy = [mk("sgy%d" % c) for c in range(NC)]
    b_qk = [mk("qk%d" % i) for i in range(5)]
    b_v = [mk("v0"), mk("v1")]
    b_mkT = [mk("mkT0"), mk("mkT1")]
    b_vx = [mk("vx0"), mk("vx1")]
    b_w = [mk("w0"), mk("w1")]
    b_U = mk("U")
    b_bA = mk("biasA")
    b_mA = mk("maskA")
    b_pt = [mk("pt%d" % i) for i in range(4)]
    b_ft = [mk("ft%d" % i) for i in range(4)]
    b_hbf = [mk("hbf0"), mk("hbf1")]
    b_rope = [mk("rope0"), mk("rope1")]
    b_c = mk("consts")
    b_ss = mk("ss")
    b_ssn = [mk("ssn0"), mk("ssn1")]
    b_lam = mk("lam")
    b_ps = [mk("ps%d" % i) for i in range(8)]
    b_scr = mk("scr")
    b_out = [mk("out%d" % t, b_x[t].grp) for t in range(NT)]

    ft2 = lambda i: ft[:, 2 * i:2 * i + 2, :].rearrange("p a b -> p (a b)")
    b_ft2 = lambda i: [b_ft[2 * i], b_ft[2 * i + 1]]

    def mm(out, lhsT, rhs, start, stop):
        return lambda e: e.matmul(out, lhsT=lhsT, rhs=rhs, start=start, stop=stop)

    def act(out, in_, func, scale=1.0, bias=None, accum_out=None):
        kw = {}
        if bias is not None:
            kw["bias"] = bias
        if accum_out is not None:
            kw["accum_out"] = accum_out
        return lambda e: e.activation(out=out, in_=in_, func=func, scale=scale, **kw)

    def tt(out, in0, in1, op):
        return lambda e: e.tensor_tensor(out=out, in0=in0, in1=in1, op=op)

    def ts(out, in0, s1, op0, s2=None, op1=None):
        if op1 is None:
            return lambda e: e.tensor_scalar(out=out, in0=in0, scalar1=s1, scalar2=None, op0=op0)
        return lambda e: e.tensor_scalar(out=out, in0=in0, scalar1=s1, scalar2=s2, op0=op0, op1=op1)

    def stt(out, in0, scalar, in1, op0, op1, accum_out=None):
        if accum_out is not None:
            return lambda e: e.scalar_tensor_tensor(out=out, in0=in0, scalar=scalar, in1=in1, op0=op0, op1=op1,
                                                    accum_out=accum_out)
        return lambda e: e.scalar_tensor_tensor(out=out, in0=in0, scalar=scalar, in1=in1, op0=op0, op1=op1)

    def cp(out, in_):
        return lambda e: e.tensor_copy(out=out, in_=in_)

    def rcp(out, in_):
        return lambda e: e.reciprocal(out=out, in_=in_)

    def mset(ap, v):
        return lambda e: e.memset(ap, v)

    P.op("pool", mset(ident, 0.0), writes=[b_c])
    P.op("pool", lambda e: e.affine_select(out=ident, in_=ident, compare_op=ALU.not_equal, fill=1.0, base=0,
                                           pattern=[[-1, 128]], channel_multiplier=1), reads=[b_c], writes=[b_c])
    P.op("pool", mset(Jm, 0.0), writes=[b_c])
    P.op("pool", lambda e: e.affine_select(out=Jm, in_=Jm, compare_op=ALU.not_equal, fill=1.0, base=-127,
                                           pattern=[[1, 128]], channel_multiplier=1), reads=[b_c], writes=[b_c])
    P.op("pool", mset(ones_bf, 1.0), writes=[b_c])
    P.op("pool", mset(ones_f, 1.0), writes=[b_c])
    P.op("pool", mset(epsb, EPS), writes=[b_c])
    P.op("pool", mset(maskA, 0.0), writes=[b_mA])
    P.op("pool", lambda e: e.affine_select(out=maskA[:, 0:128], in_=maskA[:, 0:128], compare_op=ALU.is_ge, fill=-30000.0,
                                           base=0, pattern=[[-1, 128]], channel_multiplier=1), reads=[b_mA], writes=[b_mA])
    P.op("pool", lambda e: e.affine_select(out=maskA[:, 256:384], in_=maskA[:, 256:384], compare_op=ALU.is_ge, fill=-30000.0,
                                           base=0, pattern=[[1, 128]], channel_multiplier=-1), reads=[b_mA], writes=[b_mA])
    P.op("pool", mset(vbuf[:, :, 64:128], 1.0), writes=[b_v[0], b_v[1]])
    for l in range(2):
        P.op("pool", mset(vx[l][:, :, :, 64:128], 1.0), writes=[b_vx[l]])
        P.op("pool", mset(mkT[l], 0.0), writes=[b_mkT[l]])

    P.dma("sp", tab_s, tab_d, writes=[b_c])
    P.dma("sp", smallc, smallc_d.partition_broadcast(128), writes=[b_c, b_lam], grp=b_lam.grp)

    P.op("act", act(esink[:, 0:6], esink[:, 0:6], AF.Exp), reads=[b_c], writes=[b_c])
    jk = hbf[0][:, 0:64]
    P.op("dve", stt(jk, lamt[:, 0:64], 1.0, lamt[:, 64:128], ALU.mult, ALU.mult, accum_out=lams[:, 0:1]),
         reads=[b_lam], writes=[b_hbf[0], b_lam])
    P.op("dve", stt(jk, lamt[:, 128:192], 1.0, lamt[:, 192:256], ALU.mult, ALU.mult, accum_out=lams[:, 1:2]),
         reads=[b_lam], writes=[b_hbf[0], b_lam])
    P.op("act", act(lams[:, 0:2], lams[:, 0:2], AF.Exp), reads=[b_lam], writes=[b_lam])
    P.op("dve", tt(lams[:, 2:3], lams[:, 1:2], lams[:, 0:1], ALU.subtract), reads=[b_lam], writes=[b_lam])
    P.op("dve", ts(lams[:, 2:3], lams[:, 2:3], -LAM_INIT, ALU.add), reads=[b_lam], writes=[b_lam])
    P.op("dve", stt(hbf[0][:, 0:128], smallc[:, 320:448], 1.0 - LAM_INIT, Jm, ALU.mult, ALU.mult, accum_out=gsub),
         reads=[b_lam, b_c], writes=[b_hbf[0], b_lam])
    neg_lam = lams[:, 2:3]

    P.dma("sp", ft[:, 2:4, :].rearrange("p a b -> p (a b)"), gains_d[1:2, :].partition_broadcast(128),
          writes=[b_ft[2], b_ft[3]], grp=b_ft[2].grp)
    for t in range(NT):
        P.dma("sp", xres[:, t, :], x_d[t * 128:(t + 1) * 128, :], writes=[b_x[t]])

    def build_bias_vector():
        oht_s = qk4_f32[0:32, 0:RB]
        fv = qk4_f32[0:6, 2048:2048 + RB]
        bq = b_qk[0:4]
        P.dma("sp", oht_s, oht_d, writes=bq, grp=b_qk[0].grp)
        for j, (c0, n) in enumerate([(0, 512), (512, 512), (1024, 256)]):
            P.op("pe", mm(psb[j][0:6, 0:n], tab_s, oht_s[:, c0:c0 + n], True, True), reads=[b_c] + bq, writes=[b_ps[j]])
            P.op("dve", cp(fv[:, c0:c0 + n], psb[j][0:6, 0:n]), reads=[b_ps[j]], writes=bq)
        P.dma("sp", scr.ap(), fv, reads=bq, writes=[b_scr])

    def build_U_dma(h):
        Hk = ft[:, 0:3, :].rearrange("p a b -> p (a b)")[:, 0:UW]
        P.dma("sp", Hk, bass.AP(scr, h * RB, [[1, 128], [1, UW]]), reads=[b_scr], writes=[b_ft[0], b_ft[1], b_ft[2]],
              grp=b_ft[0].grp)

    def build_U_flip():
        Hk = ft[:, 0:3, :].rearrange("p a b -> p (a b)")[:, 0:UW]
        for j, (c0, n) in enumerate([(0, 512), (512, 512), (1024, 128)]):
            bk = [7, 0, 1][j]
            P.op("pe", mm(psb[bk][:, 0:n], Jm, Hk[:, c0:c0 + n], True, True),
                 reads=[b_c, b_ft[0], b_ft[1], b_ft[2]], writes=[b_ps[bk]])
            P.op("dve", cp(U[:, c0:c0 + n], psb[bk][:, 0:n]), reads=[b_ps[bk]], writes=[b_U])

    biasbuf = [(biasA, b_bA), (U[:, 0:384], b_U)]
    b_Uhk = mk("Uhk")

    def bias_A_dma(h):
        Hk = U[:, 384:768]
        P.dma("sp", Hk, bass.AP(scr, h * RB + 384, [[1, 128], [1, 384]]), reads=[b_scr], writes=[b_Uhk])

    def bias_A_flip(h):
        Hk = U[:, 384:768]
        bb, b_bb = biasbuf[h % 2]
        P.op("pe", mm(psb[6][:, 0:384], Jm, Hk, True, True), reads=[b_c, b_Uhk], writes=[b_ps[6]])
        for jj in range(3):
            P.op("dve", tt(bb[:, jj * 128:(jj + 1) * 128], psb[6][:, 256 - 128 * jj:384 - 128 * jj],
                           maskA[:, jj * 128:(jj + 1) * 128], ALU.add), reads=[b_ps[6], b_mA], writes=[b_bb])

    wstate = {"slot": 0}

    def load_group(src3d, c0, n):
        s_ = wstate["slot"]
        wstate["slot"] ^= 1
        P.dma("pool", wbuf[s_][:, :, 0:n], src3d[:, :, c0:c0 + n], writes=[b_w[s_]])
        return s_

    psrot = {"i": 0}

    def next_bank():
        bk = [7, 0, 1, 2][psrot["i"] % 4]
        psrot["i"] += 1
        return bk

    def norm_a(src, b_src, gslot, k):
        hb = hbf[k % 2]
        bhb = b_hbf[k % 2]
        g_ap = ft2(gslot)
        sc = ssn[:, k % 2:k % 2 + 1]
        bsc = b_ssn[k % 2]
        P.op("act", act(hb, src, AF.Square, scale=float(D ** -0.5), accum_out=sc),
             reads=b_src, writes=[bhb, bsc])
        P.op("act", act(sc, sc, AF.Ln, bias=epsb), reads=[bsc, b_c], writes=[bsc])
        P.op("act", act(sc, sc, AF.Exp, scale=-0.5), reads=[bsc], writes=[bsc])
        P.op("dve", stt(hb, src, sc, g_ap, ALU.mult, ALU.mult),
             reads=b_src + [bsc] + b_ft2(gslot), writes=[bhb])

    def norm_b(dstT, b_dst, tcol, k):
        hb = hbf[k % 2]
        bhb = b_hbf[k % 2]
        bk = next_bank()
        pv = psb[bk].bitcast(BF16)
        for c in range(NC):
            P.op("pe", lambda e, c=c, pv=pv, hb=hb: e.transpose(pv[:, c * 128:(c + 1) * 128], hb[:, c * 128:(c + 1) * 128], ident),
                 reads=[bhb, b_c], writes=[b_ps[bk]])
        P.op("act", act(dstT[:, :, tcol:tcol + 128], pv.rearrange("p (c t) -> p c t", c=NC), AF.Copy),
             reads=[b_ps[bk]], writes=b_dst)

    def norm_transpose(src, b_src, gslot, dstT, b_dst, tcol, k):
        norm_a(src, b_src, gslot, k)
        norm_b(dstT, b_dst, tcol, k)

    def load_gain(idx, gslot):
        P.dma("sp", ft2(gslot), gains_d[idx:idx + 1, :].partition_broadcast(128), writes=b_ft2(gslot),
              grp=b_ft[2 * gslot].grp)

    memT = qk4[:, 0, :].rearrange("p (c t) -> p c t", c=NC)

    def mem_path(wm_slots, after_l):
        load_gain(0, 1)
        for m in range(2):
            P.dma("sp", ft2(0), mem_d[m * 128:(m + 1) * 128, :], writes=b_ft2(0), grp=b_ft[0].grp)
            norm_transpose(ft2(0), b_ft2(0), 1, memT, [b_qk[0]], m * 128, m)
        for l in range(2):
            w = wbuf[wm_slots[l]]
            bw = b_w[wm_slots[l]]
            for ch in range(2):
                bk = next_bank()
                for c in range(NC):
                    P.op("pe", mm(psb[bk][:, 0:MEM], w[:, c, ch * 128:(ch + 1) * 128], memT[:, c, 0:MEM], c == 0, c == NC - 1),
                         reads=[bw, b_qk[0]], writes=[b_ps[bk]])
                P.op("dve", cp(mkT[l][0:64, 2 * ch, :], psb[bk][0:64, 0:MEM]), reads=[b_ps[bk]], writes=[b_mkT[l]])
                P.op("dve", cp(mkT[l][64:128, 2 * ch + 1, :], psb[bk][64:128, 0:MEM]), reads=[b_ps[bk]], writes=[b_mkT[l]])
            for mt in range(2):
                bk = next_bank()
                for c in range(NC):
                    P.op("pe", mm(psb[bk][:, 0:256], memT[:, c, mt * 128:(mt + 1) * 128], w[:, c, 256:512], c == 0, c == NC - 1),
                         reads=[bw, b_qk[0]], writes=[b_ps[bk]])
                pvw = psb[bk][:, 0:256].rearrange("p (pr s d) -> p pr s d", pr=2, s=2)
                P.op("dve", cp(vx[l][:, :, mt, 0:64], pvw[:, :, 0, :]), reads=[b_ps[bk]], writes=[b_vx[l]])
                P.op("dve", cp(vx[l][:, :, mt, 128:192], pvw[:, :, 1, :]), reads=[b_ps[bk]], writes=[b_vx[l]])
            if mt == 1:
                after_l[l]()

    def phase_norm(gain_idx, preloaded=False):
        if not preloaded:
            load_gain(gain_idx, 1)
        for t in range(NT + 1):
            if t < NT:
                norm_a(xres[:, t, :], [b_x[t]], 1, t)
            if t >= 1:
                norm_b(hT, [b_hTc[(t - 1) // 4]], (t - 1) * 128, t - 1)

    def proj_fm(wslot, col0, evac):
        w = wbuf[wslot]
        for tc in range(4):
            bk = next_bank()
            for c in range(NC):
                P.op("pe", mm(psb[bk], w[:, c, col0:col0 + 128], hT[:, c, tc * 512:(tc + 1) * 512], c == 0, c == NC - 1),
                     wreads=[b_w[wslot]], reads=[b_hTc[tc]], writes=[b_ps[bk]])
            evac(bk, tc)

    def gate_phase(wd, groups, after_half=None):
        slots = []
        for gi, g in enumerate(("gate0", "gate1")):
            slots.append(load_group(wd, *groups[g]))
        for gi in range(2):
            if gi == 1 and after_half is not None:
                after_half()
            for cc in range(4):
                ch = gi * 4 + cc

                def evac(bk, tc, ch=ch):
                    P.op("act", act(sgy[:, ch, tc * 512:(tc + 1) * 512], psb[bk], AF.Silu),
                         reads=[b_ps[bk]], writes=[b_sgy[ch]])
                proj_fm(slots[gi], cc * 128, evac)

    st = {"s": 0, "p": 0}

    def run_stream(steps, sbanks=(0, 1, 2)):
        n = len(steps)
        LA = len(sbanks) - 1
        pending = []
        seq = 0
        for i in range(n + LA):
            if i < n:
                sbk = sbanks[st["s"] % len(sbanks)]
                st["s"] += 1
                pti = st["p"] % 4
                st["p"] += 1
                steps[i]["_s"] = sbk
                steps[i]["_p"] = pti
                steps[i]["qk"](sbk)
                steps[i]["sm"](sbk, pti)
            j = i - LA
            if j >= 0:
                steps[j]["pv"](steps[j]["_p"])
                fin = steps[j].get("fin")
                if fin is not None:
                    stages = fin if isinstance(fin, list) else [(1, fin)]
                    for dly, fn in stages:
                        pending.append((i + dly, seq, fn))
                        seq += 1
            due = sorted([p for p in pending if p[0] <= i])
            pending = [p for p in pending if p[0] > i]
            for _, _, fn in due:
                fn()
        for _, _, fn in sorted(pending):
            fn()

    fts = {"i": 0}

    def next_ft():
        i = fts["i"] % 4
        fts["i"] += 1
        return i

    def finalize_half(obank, half, ch, q0, nq, esink_col=None, dve_recip=False):
        ro = slice(64 * half, 64 * half + 64)
        rd = slice(64 * (1 - half), 64 * (1 - half) + 64)
        ps = psb[obank]
        fa = next_ft()
        fb = next_ft()
        A_ = ft[:, fa, :]
        B_ = ft[:, fb, :]
        if dve_recip:
            P.op("dve", rcp(B_[ro, 0:nq], ps[rd, 0:nq]), reads=[b_ps[obank]], writes=[b_ft[fb]])
        elif esink_col is not None:
            P.op("act", act(A_[rd, 0:nq], ps[rd, 0:nq], AF.Ln, bias=esink[rd, esink_col:esink_col + 1]),
                 reads=[b_ps[obank], b_c], writes=[b_ft[fa]])
        else:
            P.op("act", act(A_[rd, 0:nq], ps[rd, 0:nq], AF.Ln), reads=[b_ps[obank]], writes=[b_ft[fa]])
        if not dve_recip:
            P.op("act", act(B_[ro, 0:nq], A_[rd, 0:nq], AF.Exp, scale=-1.0), reads=[b_ft[fa]], writes=[b_ft[fb]])
        P.op("dve", tt(A_[ro, 0:nq], ps[ro, 0:nq], B_[ro, 0:nq], ALU.mult),
             reads=[b_ps[obank], b_ft[fb]], writes=[b_ft[fa]])
        dst = sgy[ro, ch, q0:q0 + nq]
        P.op("dve", tt(dst, A_[ro, 0:nq], dst, ALU.mult), reads=[b_ft[fa], b_sgy[ch]], writes=[b_sgy[ch]])

    def dense_steps(qT_ap, b_q, kT_ap, b_k, lhsT_v, b_vv, nkb, obank_of, fin_of, kcol=128):
        steps = []
        for qc in range(4):
            for kb in range(nkb):
                ob = obank_of(qc)

                def qk(sbk, qc=qc, kb=kb):
                    P.op("pe", mm(psb[sbk], kT_ap[:, kb * kcol:(kb + 1) * kcol], qT_ap[:, qc * 512:(qc + 1) * 512], True, True),
                         wreads=[b_k], reads=[b_q], writes=[b_ps[sbk]])

                def sm(sbk, pti):
                    P.op("act", act(pt[pti], psb[sbk], AF.Exp, scale=0.125), reads=[b_ps[sbk]], writes=[b_pt[pti]])

                def pv(pti, kb=kb, ob=ob):
                    P.op("pe", mm(psb[ob], lhsT_v(kb), pt[pti], kb == 0, kb == nkb - 1),
                         wreads=[b_vv], reads=[b_pt[pti]], writes=[b_ps[ob]])
                stp = dict(qk=qk, sm=sm, pv=pv)
                if kb == nkb - 1:
                    stp["fin"] = (lambda qc=qc, ob=ob: fin_of(qc, ob))
                steps.append(stp)
        return steps

    def x_attention(l):
        steps = []
        for xh in range(4):
            ch = xh // 2
            half = xh % 2
            steps += dense_steps(
                qk4[:, ch, :], b_qk[ch], mkT[l][:, xh, :], b_mkT[l],
                lambda kb, ch=ch, half=half: vx[l][:, ch, kb, 64 * half:64 * half + 128], b_vx[l], 2,
                lambda qc, xh=xh: 3 + ((xh * 4 + qc) % 4),
                lambda qc, ob, ch=ch, half=half: finalize_half(ob, half, 6 + ch, qc * 512, 512))
        run_stream(steps)

    def out_proj(wd, final, next_gain=None):
        slots = [load_group(wd, 0, 512), load_group(wd, 512, 512)]
        load_gain(3 if final else next_gain, 1)
        for t in range(NT):
            for half in range(2):
                w = wbuf[slots[half]]
                bk = next_bank()
                for c in range(NC):
                    P.op("pe", mm(psb[bk], sgy[:, c, t * 128:(t + 1) * 128], w[:, c, :], c == 0, c == NC - 1),
                         wreads=[b_sgy[c]], reads=[b_w[slots[half]]], writes=[b_ps[bk]])
                xs_ = xres[:, t, half * 512:(half + 1) * 512]
                P.op("dve", tt(xs_, psb[bk], xs_, ALU.add), reads=[b_ps[bk], b_x[t]], writes=[b_x[t]])
            if final:
                src = xres[:, t, :]
                hb = hbf[t % 2]
                sc = ssn[:, t % 2:t % 2 + 1]
                bsc = b_ssn[t % 2]
                P.op("act", act(hb, src, AF.Square, scale=float(D ** -0.5), accum_out=sc),
                     reads=[b_x[t]], writes=[b_hbf[t % 2], bsc])
                P.op("act", act(sc, sc, AF.Ln, bias=epsb), reads=[bsc, b_c], writes=[bsc])
                P.op("act", act(sc, sc, AF.Exp, scale=-0.5), reads=[bsc], writes=[bsc])
                P.op("dve", stt(src, src, sc, ft2(1), ALU.mult, ALU.mult),
                     reads=[b_x[t], bsc] + b_ft2(1), writes=[b_x[t]])
                P.dma("sp", out_d[t * 128:(t + 1) * 128, :], src, reads=[b_x[t]], writes=[b_out[t]])
            else:
                norm_a(xres[:, t, :], [b_x[t]], 1, t)
                if t >= 1:
                    norm_b(hT, [b_hTc[(t - 1) // 4]], (t - 1) * 128, t - 1)
        if not final:
            norm_b(hT, [b_hTc[3]], (NT - 1) * 128, NT - 1)

    class _Stop(Exception):
        pass

    def checkpoint(name):
        if stop_after == name + "dump":
            xflat = xres.rearrange("p t n -> p (t n)")
            for c in range(NC):
                P.op("dve", cp(xflat[:, c * S:(c + 1) * S], sgy[:, c, :]), reads=[b_sgy[c]] + b_x, writes=b_x)
        if stop_after in (name, name + "dump"):
            for t in range(NT):
                P.dma("sp", out_d[t * 128:(t + 1) * 128, :], xres[:, t, :], reads=[b_x[t]], writes=[b_out[t]])
            raise _Stop()

    def body():
        checkpoint("setup")
        phase_norm(1, preloaded=True)
        checkpoint("norm0")
        wsl = {}
        gate_phase(win_d[0], G0, after_half=lambda: wsl.__setitem__("wm0", load_group(wmem_d[0], 0, 512)))
        checkpoint("gate0")
        build_bias_vector()
        wsl["wm1"] = load_group(wmem_d[1], 0, 512)
        mem_path([wsl["wm0"], wsl["wm1"]],
                 [lambda: wsl.__setitem__("aqk", load_group(win_d[0], *G0["aqk"])),
                  lambda: wsl.__setitem__("av", load_group(win_d[0], *G0["av"]))])

        s_aqk = wsl["aqk"]
        s_av = wsl["av"]
        P.op("pool", mset(qk4[64:128, 3, :], 0.0), writes=[b_qk[3]])
        P.op("pool", mset(qk4[0:64, 4, :], 0.0), writes=[b_qk[4]])
        for j in range(4):
            def evac(bk, tc, j=j):
                if j < 3:
                    P.op("dve", cp(qk4[:, j, tc * 512:(tc + 1) * 512], psb[bk]), reads=[b_ps[bk]], writes=[b_qk[j]])
                else:
                    P.op("dve", cp(qk4[0:64, 3, tc * 512:(tc + 1) * 512], psb[bk][0:64, :]), reads=[b_ps[bk]], writes=[b_qk[3]])
                    P.op("dve", cp(qk4[64:128, 4, tc * 512:(tc + 1) * 512], psb[bk][64:128, :]), reads=[b_ps[bk]], writes=[b_qk[4]])
            proj_fm(s_aqk, j * 128, evac)

        def proj_v_tm(wslot, dst_lo, dst_hi):
            w = wbuf[wslot]
            for t in range(NT):
                bk = next_bank()
                for c in range(NC):
                    P.op("pe", mm(psb[bk][:, 0:128], hT[:, c, t * 128:(t + 1) * 128], w[:, c, 0:128], c == 0, c == NC - 1),
                         reads=[b_w[wslot], b_hTc[t // 4]], writes=[b_ps[bk]])
                P.op("dve", cp(vbuf[:, t, 0:64], psb[bk][:, 0:64]), reads=[b_ps[bk]], writes=[b_v[0], b_v[1]])
                P.op("dve", cp(vbuf[:, t, 128:192], psb[bk][:, 64:128]), reads=[b_ps[bk]], writes=[b_v[0], b_v[1]])
        proj_v_tm(s_av, 0, 128)

        def a_head_steps(h):
            j = h % 3
            half = h // 3
            rows = slice(64 * half, 64 * half + 64)
            steps = []
            for qb in range(NT):
                kbs = [kb for kb in (qb - 1, qb, qb + 1) if 0 <= kb < NT]
                c0 = (kbs[0] - qb + 1) * 128
                n = len(kbs) * 128
                ob = 3 + (qb // 4) % 3
                bb, b_bb = biasbuf[h % 2]

                def qk(sbk, qb=qb, kbs=kbs):
                    for kb in kbs:
                        jj = kb - qb + 1
                        P.op("pe", mm(psb[sbk][:, jj * 128:(jj + 1) * 128], qk4[:, 3 + half, kb * 128:(kb + 1) * 128],
                                      qk4[:, j, qb * 128:(qb + 1) * 128], True, True),
                             wreads=[b_qk[3 + half]], reads=[b_qk[j]], writes=[b_ps[sbk]])

                def sm(sbk, pti, c0=c0, n=n):
                    P.op("dve", stt(psb[sbk][:, c0:c0 + n], psb[sbk][:, c0:c0 + n], 0.125, bb[:, c0:c0 + n], ALU.mult, ALU.add),
                         reads=[b_ps[sbk], b_bb], writes=[b_ps[sbk]])
                    P.op("act", act(pt[pti][:, c0:c0 + n], psb[sbk][:, c0:c0 + n], AF.Exp), reads=[b_ps[sbk]], writes=[b_pt[pti]])

                def pv(pti, qb=qb, kbs=kbs, ob=ob):
                    for idx, kb in enumerate(kbs):
                        jj = kb - qb + 1
                        P.op("pe", mm(psb[ob][:, (qb % 4) * 128:(qb % 4) * 128 + 128], vbuf[:, kb, 64 * half:64 * half + 128],
                                      pt[pti][:, jj * 128:(jj + 1) * 128], idx == 0, idx == len(kbs) - 1),
                             wreads=[b_v[0]], reads=[b_pt[pti]], writes=[b_ps[ob]])
                stp = dict(qk=qk, sm=sm, pv=pv)
                if qb % 4 == 3:
                    stp["fin"] = [(1, (lambda qc=qb // 4, ob=ob: finalize_half(ob, half, j, qc * 512, 512, esink_col=h)))]
                if qb == 5 and h + 1 < 6:
                    stp["fin"] = [(0, (lambda: bias_A_flip(h + 1)))]
                steps.append(stp)
            return steps

        checkpoint("aproj")
        bias_A_dma(0)
        bias_A_flip(0)
        for h in range(6):
            if h + 1 < 6:
                bias_A_dma(h + 1)
            run_stream(a_head_steps(h), sbanks=(0, 1, 2, 7))
        checkpoint("A")

        s_bqk = load_group(win_d[0], *G0["bqk"])
        s_bv = load_group(win_d[0], *G0["bv"])
        P.dma("sp", ft[:, 3, :], qkn_d.partition_broadcast(128), writes=[b_ft[3]])
        qkn_t = psb[6]
        P.op("dve", cp(qkn_t, ft[:, 3, :]), reads=[b_ft[3]], writes=[b_ps[6]])
        xs_of = lambda t: (ft[:, 0, :], b_ft[0]) if t % 2 == 0 else (ft[:, 3, :], b_ft[3])
        b_ss2 = [b_ss, b_ssn[0]]
        ss_of = lambda t: (ss[:, 0:8], b_ss) if t % 2 == 0 else (ss8b, b_ssn[0])

        def b_stage1(t):
            rs = ropes[t % 2]
            P.dma("sp", rs, rope_d[t * 128:(t + 1) * 128, :], writes=[b_rope[t % 2]])
            bk = [0, 1][t % 2]
            for c in range(NC):
                P.op("pe", mm(psb[bk], hT[:, c, t * 128:(t + 1) * 128], wbuf[s_bqk][:, c, :], c == 0, c == NC - 1),
                     reads=[b_w[s_bqk], b_hTc[t // 4]], writes=[b_ps[bk]])
            bk2 = [2, 7][t % 2]
            for c in range(NC):
                P.op("pe", mm(psb[bk2][:, 0:128], hT[:, c, t * 128:(t + 1) * 128], wbuf[s_bv][:, c, 0:128], c == 0, c == NC - 1),
                     reads=[b_w[s_bv], b_hTc[t // 4]], writes=[b_ps[bk2]])
            xs_, bxs = xs_of(t)
            P.op("act", act(xs_, psb[bk], AF.Copy), reads=[b_ps[bk]], writes=[bxs])
            P.op("act", act(vbuf[:, t, 0:64], psb[bk2][:, 0:64], AF.Copy), reads=[b_ps[bk2]], writes=[b_v[0], b_v[1]])
            P.op("act", act(vbuf[:, t, 128:192], psb[bk2][:, 64:128], AF.Copy), reads=[b_ps[bk2]], writes=[b_v[0], b_v[1]])

        def b_stage2a(t):
            xs_, bxs = xs_of(t)
            ssv, bssv = ss_of(t)
            sqp = psb[5]
            P.op("dve", tt(sqp, xs_, xs_, ALU.mult), reads=[bxs], writes=[b_ps[5]])
            P.op("dve", lambda e, ssv=ssv: e.tensor_reduce(out=ssv, in_=sqp.rearrange("p (h d) -> p h d", h=8), axis=AX.X, op=ALU.add),
                 reads=[b_ps[5]], writes=[bssv])
            P.op("act", act(ssv, ssv, AF.Ln, scale=1.0 / 64, bias=epsb), reads=[bssv, b_c], writes=[bssv])
            P.op("act", act(ssv, ssv, AF.Exp, scale=-0.5), reads=[bssv], writes=[bssv])

        def b_stage2b(t):
            rs = ropes[t % 2]
            xs_, bxs = xs_of(t)
            ssv, bssv = ss_of(t)
            sq_ = ft[:, 1, :]
            t2_ = ft[:, 2, :]
            x3 = xs_.rearrange("p (h d) -> p h d", h=8)
            P.op("dve", tt(x3, x3, ssv.unsqueeze(2).to_broadcast([128, 8, 64]), ALU.mult),
                 reads=[bxs, bssv], writes=[bxs])
            P.op("dve", tt(xs_, xs_, qkn_t, ALU.mult), reads=[bxs, b_ps[6]], writes=[bxs])
            cc = rs[:, 0:64].unsqueeze(1).to_broadcast([128, 8, 64])
            P.op("dve", tt(sq_.rearrange("p (h d) -> p h d", h=8), x3, cc, ALU.mult),
                 reads=[bxs, b_rope[t % 2]], writes=[b_ft[1]])
            x4 = xs_.rearrange("p (h a s d) -> p h a s d", h=8, a=2, s=2)
            t4 = t2_.rearrange("p (h a s d) -> p h a s d", h=8, a=2, s=2)
            s4 = rs[:, 64:128].rearrange("p (a s d) -> p a s d", a=2, s=2)
            for s_ in range(2):
                P.op("dve", tt(t4[:, :, :, s_, :], x4[:, :, :, 1 - s_, :],
                               s4[:, :, s_, :].unsqueeze(1).to_broadcast([128, 8, 2, 16]), ALU.mult),
                     reads=[bxs, b_rope[t % 2]], writes=[b_ft[2]])
            hb = hbf[t % 2]
            P.op("dve", tt(hb[:, 0:512], sq_, t2_, ALU.add), reads=[b_ft[1], b_ft[2]], writes=[b_hbf[t % 2]])

        def b_stage3(t):
            hb = hbf[t % 2]
            bk3 = [3, 4][t % 2]
            pvv = psb[bk3].bitcast(BF16)
            for c in range(4):
                P.op("pe", lambda e, c=c, pvv=pvv, hb=hb: e.transpose(pvv[:, c * 128:(c + 1) * 128], hb[:, c * 128:(c + 1) * 128], ident),
                     reads=[b_hbf[t % 2], b_c], writes=[b_ps[bk3]])
            P.op("act", act(qk4[:, 0:3, t * 128:(t + 1) * 128], pvv[:, 0:384].rearrange("p (c t) -> p c t", c=3), AF.Copy),
                 reads=[b_ps[bk3]], writes=b_qk[0:3])
            P.op("act", act(qk4[0:64, 3, t * 128:(t + 1) * 128], pvv[0:64, 384:512], AF.Copy), reads=[b_ps[bk3]], writes=[b_qk[3]])
            P.op("act", act(qk4[64:128, 4, t * 128:(t + 1) * 128], pvv[64:128, 384:512], AF.Copy), reads=[b_ps[bk3]], writes=[b_qk[4]])

        b_stage1(0)
        b_stage2a(0)
        for t in range(NT):
            if t + 1 < NT:
                b_stage1(t + 1)
            b_stage2b(t)
            if t + 1 < NT:
                b_stage2a(t + 1)
            b_stage3(t)

        checkpoint("bprep")
        steps = []
        for h in range(6):
            j = h % 3
            half = h // 3
            steps += dense_steps(
                qk4[:, j, :], b_qk[j], qk4[:, 3 + half, :], b_qk[3 + half],
                lambda kb, half=half: vbuf[:, kb, 64 * half:64 * half + 128], b_v[0], NT,
                lambda qc, h=h: 3 + ((h * 4 + qc) % 4),
                lambda qc, ob, j=j, half=half: finalize_half(ob, half, 3 + j, qc * 512, 512, dve_recip=True))
        run_stream(steps)
        checkpoint("B")

        def x_proj(wd, grp):
            s_x = load_group(wd, *grp)
            for ch in range(2):
                def evac(bk, tc, ch=ch):
                    P.op("dve", cp(qk4[:, ch, tc * 512:(tc + 1) * 512], psb[bk]), reads=[b_ps[bk]], writes=[b_qk[ch]])
                proj_fm(s_x, ch * 128, evac)

        x_proj(win_d[0], G0["xq"])
        x_attention(0)
        checkpoint("X0")
        out_proj(wout_d[0], final=(n_layers == 1), next_gain=2)
        checkpoint("out0")

        if n_layers == 2:
            gate_phase(win_d[1], G1)
            for h in range(6):
                build_U_dma(h)
                s_c = load_group(win_d[1], *G1["c%d" % h])
                qi = 0 if h % 2 == 0 else 3
                if h == 0:
                    P.op("pool", mset(qk4[64:128, 1, :], 0.0), writes=[b_qk[1]])
                    P.op("pool", mset(qk4[0:64, 2, :], 0.0), writes=[b_qk[2]])
                vsl = slice((h % 2) * 128, (h % 2) * 128 + 128)
                bvh = b_v[h % 2]
                def evac_q(bk, tc):
                    P.op("dve", cp(qk4[:, qi, tc * 512:(tc + 1) * 512], psb[bk]), reads=[b_ps[bk]], writes=[b_qk[qi]])

                def evac_k(bk, tc):
                    P.op("dve", cp(qk4[0:64, 1, tc * 512:(tc + 1) * 512], psb[bk][0:64, :]), reads=[b_ps[bk]], writes=[b_qk[1]])
                    P.op("dve", cp(qk4[64:128, 2, tc * 512:(tc + 1) * 512], psb[bk][64:128, :]), reads=[b_ps[bk]], writes=[b_qk[2]])
                proj_fm(s_c, 0, evac_q)
                proj_fm(s_c, 128, evac_k)
                for t in range(NT):
                    bk = next_bank()
                    for c in range(NC):
                        P.op("pe", mm(psb[bk][:, 0:128], hT[:, c, t * 128:(t + 1) * 128], wbuf[s_c][:, c, 256:384], c == 0, c == NC - 1),
                             reads=[b_w[s_c], b_hTc[t // 4]], writes=[b_ps[bk]])
                    P.op("act", act(vbuf[:, t, vsl], psb[bk][:, 0:128], AF.Copy), reads=[b_ps[bk]], writes=[bvh])
                build_U_flip()
                steps = []
                for qc in range(4):
                    for s_ in range(2):
                        rows = slice(64 * s_, 64 * s_ + 64)
                        ob, db = (3, 4) if s_ == 0 else (5, 6)
                        for kb in range(NT):
                            delta = kb - 4 * qc

                            def qk(sbk, kb=kb, qc=qc, s_=s_):
                                P.op("pe", mm(psb[sbk], qk4[:, 1 + s_, kb * 128:(kb + 1) * 128], qk4[:, qi, qc * 512:(qc + 1) * 512], True, True),
                                     wreads=[b_qk[1 + s_]], reads=[b_qk[qi]], writes=[b_ps[sbk]])

                            def sm(sbk, pti, delta=delta):
                                if -1 <= delta <= 4:
                                    m0 = 512 - 128 * delta
                                    P.op("dve", stt(psb[sbk], psb[sbk], 0.125, U[:, m0:m0 + 512], ALU.mult, ALU.add),
                                         reads=[b_ps[sbk], b_U], writes=[b_ps[sbk]])
                                    P.op("act", act(pt[pti], psb[sbk], AF.Exp), reads=[b_ps[sbk]], writes=[b_pt[pti]])
                                else:
                                    col = UW - 1 if delta <= -2 else 0
                                    P.op("act", act(pt[pti], psb[sbk], AF.Exp, scale=0.125, bias=U[:, col:col + 1]),
                                         reads=[b_ps[sbk], b_U], writes=[b_pt[pti]])

                            def pv(pti, kb=kb, ob=ob, db=db):
                                P.op("pe", mm(psb[ob], vbuf[:, kb, vsl], pt[pti], kb == 0, kb == NT - 1),
                                     wreads=[bvh], reads=[b_pt[pti]], writes=[b_ps[ob]])
                                P.op("pe", mm(psb[db], ones_bf, pt[pti], kb == 0, kb == NT - 1),
                                     wreads=[b_c], reads=[b_pt[pti]], writes=[b_ps[db]])
                            stp = dict(qk=qk, sm=sm, pv=pv)
                            if kb == NT - 1:
                                if s_ == 0:
                                    def fin(ob=ob, db=db):
                                        P.op("act", act(ft[:, 0, :], psb[db], AF.Ln), reads=[b_ps[db]], writes=[b_ft[0]])
                                        P.op("act", act(ft[:, 0, :], ft[:, 0, :], AF.Exp, scale=-1.0), reads=[b_ft[0]], writes=[b_ft[0]])
                                        P.op("dve", tt(ft[:, 1, :], psb[ob], ft[:, 0, :], ALU.mult),
                                             reads=[b_ps[ob], b_ft[0]], writes=[b_ft[1]])
                                else:
                                    r2, t1, d_, q_ = ft[:, 0, :], ft[:, 1, :], ft[:, 2, :], ft[:, 3, :]

                                    def fin(ob=ob, db=db):
                                        P.op("act", act(r2, psb[db], AF.Ln), reads=[b_ps[db]], writes=[b_ft[0]])
                                        P.op("act", act(r2, r2, AF.Exp, scale=-1.0), reads=[b_ft[0]], writes=[b_ft[0]])
                                        P.op("dve", tt(d_, psb[ob], r2, ALU.mult), reads=[b_ps[ob], b_ft[0]], writes=[b_ft[2]])
                                        P.op("dve", stt(d_, d_, neg_lam, t1, ALU.mult, ALU.add),
                                             reads=[b_ft[2], b_ft[1], b_lam], writes=[b_ft[2]])
                                        P.op("dve", tt(q_, d_, d_, ALU.mult), reads=[b_ft[2]], writes=[b_ft[3]])

                                    def finB(db=db):
                                        P.op("pe", mm(psb[db], ones_f, q_, True, True), reads=[b_c, b_ft[3]], writes=[b_ps[db]])
                                        P.op("act", act(q_, psb[db], AF.Ln, scale=1.0 / 128, bias=epsb),
                                             reads=[b_ps[db], b_c], writes=[b_ft[3]])
                                        P.op("act", act(q_, q_, AF.Exp, scale=-0.5), reads=[b_ft[3]], writes=[b_ft[3]])

                                    def finC(qc=qc, h=h):
                                        P.op("dve", tt(d_, d_, q_, ALU.mult), reads=[b_ft[2], b_ft[3]], writes=[b_ft[2]])
                                        dst = sgy[:, h, qc * 512:(qc + 1) * 512]
                                        P.op("dve", stt(dst, d_, gsub, dst, ALU.mult, ALU.mult),
                                             reads=[b_ft[2], b_lam, b_sgy[h]], writes=[b_sgy[h]])
                                stp["fin"] = [(1, fin)] if s_ == 0 else [(1, fin), (9, finB), (12, finC)]
                            steps.append(stp)
                run_stream(steps, sbanks=(0, 1, 2, 7))
            x_proj(win_d[1], G1["xq"])
            x_attention(1)
            out_proj(wout_d[1], final=True)


    try:
        body()
    except _Stop:
        pass
    stats = P.emit()
    return nc, stats


_CACHE = {}


def _prep_weights(inputs):
    f = lambda a: np.asarray(a, dtype=np.float32)
    oht, rope = _host_consts()
    w_in0 = _pc(f(inputs["even_w_in"])[0][:, _cols0()])
    w_in1 = _pc(f(inputs["odd_w_in"])[0][:, _cols1()])
    w_out0 = _pc(f(inputs["even_w_out"])[0][MYORDER0, :])
    w_out1 = _pc(f(inputs["odd_w_out"])[0])
    w_mem0 = _pc(f(inputs["even_w_mem_kv"])[0])
    w_mem1 = _pc(f(inputs["odd_w_mem_kv"])[0])
    gains = np.stack([f(inputs["mem_norm"]), f(inputs["even_norm"])[0], f(inputs["odd_norm"])[0],
                      f(inputs["final_norm"])]).astype(np.float32)
    qn, kn = f(inputs["even_q_norm"])[0], f(inputs["even_k_norm"])[0]
    qkn = np.concatenate([np.tile(qn, 6), np.tile(kn, 2)])[None, :].astype(np.float32)
    lamv = np.concatenate([f(inputs["odd_lambda_q1"])[0], f(inputs["odd_lambda_k1"])[0],
                           f(inputs["odd_lambda_q2"])[0], f(inputs["odd_lambda_k2"])[0]])[None, :].astype(np.float32)
    smallc = np.zeros((1, 512), np.float32)
    smallc[0, 0:6] = f(inputs["even_sink"])[0]
    smallc[0, 64:320] = lamv[0]
    smallc[0, 320:448] = f(inputs["odd_subln"])[0][::-1]
    return dict(tab=np.ascontiguousarray(f(inputs["rel_bias"])), oht=oht, rope=rope, gains=gains,
                w_in0=w_in0, w_in1=w_in1, w_mem0=w_mem0, w_mem1=w_mem1, w_out0=w_out0, w_out1=w_out1,
                smallc=smallc, qkn=qkn)


def kernel(**inputs):
    if "nc" not in _CACHE:
        _CACHE["nc"], _CACHE["stats"] = build_program()
    nc = _CACHE["nc"]
    shared = _prep_weights(inputs)
    x = np.asarray(inputs["x"], dtype=np.float32)
    mem = np.asarray(inputs["mem"], dtype=np.float32)
    in_maps = []
    for b in range(8):
        m = dict(shared)
        m["x"] = np.ascontiguousarray(x[b])
        m["mem"] = np.ascontiguousarray(mem[b])
        in_maps.append(m)
    res = run_bass_kernel_spmd(nc, in_maps, core_ids=list(range(8)))
    return np.stack([np.asarray(r["out"], dtype=np.float32) for r in res.results], axis=0)
```

```python
import math
import numpy as np
import concourse.bass as bass
import concourse.mybir as mybir
from concourse.bass_utils import run_bass_kernel_spmd

F32 = mybir.dt.float32
BF16 = mybir.dt.bfloat16
AF = mybir.ActivationFunctionType
ALU = mybir.AluOpType
AX = mybir.AxisListType

S = 2048
D = 1024
NT = 16
NC = 8
MEM = 256
EPS = 1e-6
RB = 1280
UW = 1152
LAM_INIT = 0.8 - 0.6 * math.exp(-0.3 * 1)


class Buf:
    __slots__ = ("name", "last_w", "readers", "grp")

    def __init__(self, name, grp=None):
        self.name = name
        self.last_w = None
        self.readers = []
        self.grp = grp if grp is not None else DGroup(name)


class DGroup:
    __slots__ = ("name", "sem", "count", "final")

    def __init__(self, name, final=False):
        self.name = name
        self.sem = None
        self.count = 0
        self.final = final


class Op:
    __slots__ = ("eng", "fn", "deps", "is_dma", "grp", "needs_inc", "val", "wdeps", "iwait")

    def __init__(self, eng, fn, is_dma=False, grp=None):
        self.eng = eng
        self.fn = fn
        self.deps = []
        self.wdeps = []
        self.iwait = False
        self.is_dma = is_dma
        self.grp = grp
        self.needs_inc = False
        self.val = None


class Prog:
    ENGS = ("pe", "act", "dve", "pool", "sp")

    def __init__(self, nc):
        self.nc = nc
        self.ops = []
        self.h = {"pe": nc.tensor, "act": nc.scalar, "dve": nc.vector, "pool": nc.gpsimd, "sp": nc.sync}
        self.out_groups = []

    def _add(self, op, reads, writes, wreads=()):
        deps = []
        for b in wreads:
            if b.last_w is not None and b.last_w is not op:
                op.wdeps.append(b.last_w)
        reads = list(reads) + list(wreads)
        for b in reads:
            if b.last_w is not None:
                deps.append(b.last_w)
        for b in writes:
            if b.last_w is not None:
                deps.append(b.last_w)
            deps.extend(b.readers)
        seen = set()
        for d in deps:
            if id(d) in seen or d is op:
                continue
            seen.add(id(d))
            if d.eng == "pe" and op.eng == "pe" and not d.is_dma and not op.is_dma:
                continue
            op.deps.append(d)
        for b in reads:
            if not op.is_dma:
                b.readers = [r for r in b.readers if r.is_dma or r.eng != op.eng]
            b.readers.append(op)
        for b in writes:
            b.last_w = op
            b.readers = []
        self.ops.append(op)
        return op

    def op(self, eng, fn, reads=(), writes=(), wreads=None):
        o = Op(eng, fn)
        if wreads is not None and eng == "pe":
            o.iwait = True
            return self._add(o, list(reads), list(writes), list(wreads))
        return self._add(o, list(reads), list(writes))

    def dma(self, eng, out_ap, in_ap, reads=(), writes=(), grp=None):
        writes = list(writes)
        if grp is None:
            grp = writes[0].grp
        fn = lambda e: e.dma_start(out=out_ap, in_=in_ap)
        return self._add(Op(eng, fn, is_dma=True, grp=grp), list(reads), writes)

    def emit(self):
        nc = self.nc
        for op in self.ops:
            for d in op.deps:
                d.needs_inc = True
            for d in op.wdeps:
                d.needs_inc = True
        esem = {e: nc.alloc_semaphore("sem_" + e) for e in self.ENGS}
        cnt = {e: 0 for e in self.ENGS}
        for op in self.ops:
            if op.is_dma:
                g = op.grp
                if g.sem is None:
                    g.sem = nc.alloc_semaphore("dsem_" + g.name)
                g.count += 16
                op.val = (g, g.count)
            elif op.needs_inc:
                cnt[op.eng] += 1
                op.val = (esem[op.eng], cnt[op.eng])
        waited = {e: {} for e in self.ENGS}
        nwaits = 0
        for op in self.ops:
            e = self.h[op.eng]
            w = waited[op.eng]
            need = {}
            wkeys = set()
            for d in op.wdeps:
                wkeys.add((d.val[0].sem if d.is_dma else d.val[0]).num)
            for d in op.deps:
                if d.is_dma:
                    g, val = d.val
                    sem = g.sem
                    if g.final:
                        val = g.count
                else:
                    sem, val = d.val
                k = sem.num
                if w.get(k, 0) >= val:
                    continue
                if k not in need or need[k][1] < val:
                    need[k] = (sem, val)
            items = list(need.items())
            attach = None
            if op.iwait and items:
                cand = [it for it in items if it[0] not in wkeys]
                if cand:
                    attach = cand[-1]
                    items = [it for it in items if it is not attach]
            for k, (sem, val) in items:
                e.wait_ge(sem, val)
                w[k] = val
                nwaits += 1
            ins = op.fn(e)
            if attach is not None:
                k, (sem, val) = attach
                ins._wait_ge(sem, val)
                w[k] = val
                nwaits += 1
            if op.is_dma:
                ins.then_inc(op.val[0].sem, 16)
            elif op.needs_inc:
                ins.then_inc(op.val[0], 1)
        seen_g = set()
        for op in self.ops:
            if op.is_dma and id(op.grp) not in seen_g:
                seen_g.add(id(op.grp))
                self.h["sp"].wait_ge(op.grp.sem, op.grp.count)
        return dict(nops=len(self.ops), nwaits=nwaits, cnt=cnt)


def _t5_bucket_np(rel):
    nb, max_exact = 16, 8
    ret = np.where(rel > 0, nb, 0)
    n = np.abs(rel)
    nf = np.maximum(n, 1).astype(np.float32)
    large = max_exact + (np.log(nf / np.float32(max_exact)) / np.float32(math.log(128 / max_exact))
                         * np.float32(nb - max_exact)).astype(np.int32)
    large = np.minimum(large, nb - 1)
    return ret + np.where(n < max_exact, n, large)


MYORDER0 = np.concatenate(
    [np.concatenate([np.arange(64 * j, 64 * j + 64), np.arange(64 * (j + 3), 64 * (j + 3) + 64)]) for j in range(3)]
    + [384 + np.concatenate([np.arange(64 * j, 64 * j + 64), np.arange(64 * (j + 3), 64 * (j + 3) + 64)]) for j in range(3)]
    + [np.arange(768, 1024)])

G0 = dict(gate0=(0, 512), gate1=(512, 512), aqk=(1024, 512), av=(1536, 128), bqk=(1664, 512), bv=(2176, 128),
          xq=(2304, 256))
G1 = dict(gate0=(0, 512), gate1=(512, 512), xq=(1024 + 6 * 384, 256))
for _h in range(6):
    G1["c%d" % _h] = (1024 + 384 * _h, 384)


def _cols0():
    pair = lambda base: np.concatenate(
        [np.concatenate([base + np.arange(64 * j, 64 * j + 64), base + np.arange(64 * (j + 3), 64 * (j + 3) + 64)])
         for j in range(3)])
    return np.concatenate([1536 + MYORDER0, pair(0), np.arange(384, 512), np.arange(512, 640),
                           pair(640), np.arange(1024, 1152), np.arange(1152, 1280), np.arange(1280, 1536)])


def _cols1():
    cols = [2560 + np.arange(1024)]
    for h in range(6):
        cols += [64 * h + np.arange(64), 384 + 64 * h + np.arange(64), 768 + 64 * h + np.arange(64),
                 1152 + 64 * h + np.arange(64), 1536 + 128 * h + np.arange(128)]
    cols.append(np.arange(2304, 2560))
    return np.concatenate(cols)


def _pc(w):
    k, n = w.shape
    return np.ascontiguousarray(w.reshape(k // 128, 128, n).transpose(1, 0, 2))


def _host_consts():
    i = np.arange(RB)
    b = _t5_bucket_np(639 - i)
    oht = np.zeros((32, RB), np.float32)
    oht[b, i] = 1.0
    rows = S // 64
    row = np.repeat(np.arange(rows), 64).astype(np.float64)
    col = np.tile(np.arange(64), rows).astype(np.float64)
    inv = 1.0 / (10000.0 ** (np.arange(0, 32, 2, dtype=np.float32) / np.float32(32))).astype(np.float32)
    ar = (row[:, None].astype(np.float32) * inv).astype(np.float32).astype(np.float64)
    ac = (col[:, None].astype(np.float32) * inv).astype(np.float32).astype(np.float64)
    c0, s0, c1, s1 = np.cos(ar), np.sin(ar), np.cos(ac), np.sin(ac)
    rope = np.concatenate([c0, c0, c1, c1, -s0, s0, -s1, s1], axis=1).astype(np.float32)
    return oht, rope


def build_program(n_layers=2, stop_after=None):
    nc = bass.Bass("TRN2", target_bir_lowering=False)
    P = Prog(nc)
    din = lambda name, shape: nc.dram_tensor(name, list(shape), F32, kind="ExternalInput").ap()
    x_d = din("x", [S, D])
    mem_d = din("mem", [MEM, D])
    tab_d = din("tab", [32, 6])
    oht_d = din("oht", [32, RB])
    rope_d = din("rope", [S, 128])
    gains_d = din("gains", [4, D])
    win_d = [din("w_in0", [128, NC, 2560]), din("w_in1", [128, NC, 3584])]
    wmem_d = [din("w_mem0", [128, NC, 512]), din("w_mem1", [128, NC, 512])]
    wout_d = [din("w_out0", [128, NC, D]), din("w_out1", [128, NC, D])]
    smallc_d = din("smallc", [1, 512])
    qkn_d = din("qkn", [1, 512])
    out_d = nc.dram_tensor("out", [S, D], F32, kind="ExternalOutput").ap()
    scr = nc.dram_tensor("scr", [6, RB], F32, kind="Internal")

    sb = lambda name, shape, dt: nc.alloc_sbuf_tensor(name, list(shape), dt).ap()
    xres = sb("xres", [128, NT, D], F32)
    hT_flat = sb("hT", [128, NC * S], BF16)
    hT = hT_flat.rearrange("p (c t) -> p c t", c=NC)
    hT_f32 = hT_flat.bitcast(F32)
    sgy = sb("sgy", [128, NC, S], BF16)
    qk4_flat = sb("qk4", [128, 5 * S], BF16)
    qk4 = qk4_flat.rearrange("p (i t) -> p i t", i=5)
    qk4_f32 = qk4_flat.bitcast(F32)
    vbuf = sb("vbuf", [128, NT, 256], BF16)
    mkT = [sb("mkT%d" % l, [128, 4, MEM], BF16) for l in range(2)]
    vx = [sb("vx%d" % l, [128, 2, 2, 192], BF16) for l in range(2)]
    wbuf = [sb("wbuf%d" % i, [128, NC, 512], BF16) for i in range(2)]
    U = sb("U", [128, UW], F32)
    biasA = sb("biasA", [128, 384], F32)
    maskA = sb("maskA", [128, 384], F32)
    pt = [sb("pt%d" % i, [128, 512], BF16) for i in range(4)]
    ft = sb("ft", [128, 4, 512], F32)
    hbf = [sb("hbf%d" % i, [128, D], BF16) for i in range(2)]
    ropes = [sb("rope%d" % i, [128, 128], F32) for i in range(2)]
    ident = sb("ident", [128, 128], BF16)
    ones_bf = sb("ones_bf", [128, 128], BF16)
    ones_f = sb("ones_f", [128, 128], F32)
    Jm = sb("Jm", [128, 128], F32)
    tab_s = sb("tab_s", [32, 6], F32)
    epsb = sb("epsb", [128, 1], F32)
    ss = sb("ss", [128, 8], F32)
    ssn = sb("ssn", [128, 4], F32)
    ss8b = sb("ss8b", [128, 8], F32)
    smallc = sb("smallc_s", [128, 512], F32)
    esink = smallc[:, 0:8]
    lamt = smallc[:, 64:320]
    lams = sb("lams", [128, 4], F32)
    gsub = sb("gsub", [128, 1], F32)
    psb = [nc.alloc_psum_tensor("ps%d" % i, [128, 512], F32).ap() for i in range(8)]

    B = {}

    def mk(name, grp=None):
        B[name] = Buf(name, grp)
        return B[name]

    b_x = [mk("x%d" % t) for t in range(NT)]
    b_hTc = [mk("hT%d" % i) for i in range(4)]
    b_sgy = [mk("sgy%d" % c) for c in range(NC)]
    b_qk = [mk("qk%d" % i) for i in range(5)]
    b_v = [mk("v0"), mk("v1")]
    b_mkT = [mk("mkT0"), mk("mkT1")]
    b_vx = [mk("vx0"), mk("vx1")]
    b_w = [mk("w0"), mk("w1")]
    b_U = mk("U")
    b_bA = mk("biasA")
    b_mA = mk("maskA")
    b_pt = [mk("pt%d" % i) for i in range(4)]
    b_ft = [mk("ft%d" % i) for i in range(4)]
    b_hbf = [mk("hbf0"), mk("hbf1")]
    b_rope = [mk("rope0"), mk("rope1")]
    b_c = mk("consts")
    b_ss = mk("ss")
    b_ssn = [mk("ssn0"), mk("ssn1")]
    b_lam = mk("lam")
    b_ps = [mk("ps%d" % i) for i in range(8)]
    b_scr = mk("scr")
    b_out = [mk("out%d" % t, b_x[t].grp) for t in range(NT)]

    ft2 = lambda i: ft[:, 2 * i:2 * i + 2, :].rearrange("p a b -> p (a b)")
    b_ft2 = lambda i: [b_ft[2 * i], b_ft[2 * i + 1]]

    def mm(out, lhsT, rhs, start, stop):
        return lambda e: e.matmul(out, lhsT=lhsT, rhs=rhs, start=start, stop=stop)

    def act(out, in_, func, scale=1.0, bias=None, accum_out=None):
        kw = {}
        if bias is not None:
            kw["bias"] = bias
        if accum_out is not None:
            kw["accum_out"] = accum_out
        return lambda e: e.activation(out=out, in_=in_, func=func, scale=scale, **kw)

    def tt(out, in0, in1, op):
        return lambda e: e.tensor_tensor(out=out, in0=in0, in1=in1, op=op)

    def ts(out, in0, s1, op0, s2=None, op1=None):
        if op1 is None:
            return lambda e: e.tensor_scalar(out=out, in0=in0, scalar1=s1, scalar2=None, op0=op0)
        return lambda e: e.tensor_scalar(out=out, in0=in0, scalar1=s1, scalar2=s2, op0=op0, op1=op1)

    def stt(out, in0, scalar, in1, op0, op1, accum_out=None):
        if accum_out is not None:
            return lambda e: e.scalar_tensor_tensor(out=out, in0=in0, scalar=scalar, in1=in1, op0=op0, op1=op1,
                                                    accum_out=accum_out)
        return lambda e: e.scalar_tensor_tensor(out=out, in0=in0, scalar=scalar, in1=in1, op0=op0, op1=op1)

    def cp(out, in_):
        return lambda e: e.tensor_copy(out=out, in_=in_)

    def rcp(out, in_):
        return lambda e: e.reciprocal(out=out, in_=in_)

    def mset(ap, v):
        return lambda e: e.memset(ap, v)

    P.op("pool", mset(ident, 0.0), writes=[b_c])
    P.op("pool", lambda e: e.affine_select(out=ident, in_=ident, compare_op=ALU.not_equal, fill=1.0, base=0,
                                           pattern=[[-1, 128]], channel_multiplier=1), reads=[b_c], writes=[b_c])
    P.op("pool", mset(Jm, 0.0), writes=[b_c])
    P.op("pool", lambda e: e.affine_select(out=Jm, in_=Jm, compare_op=ALU.not_equal, fill=1.0, base=-127,
                                           pattern=[[1, 128]], channel_multiplier=1), reads=[b_c], writes=[b_c])
    P.op("pool", mset(ones_bf, 1.0), writes=[b_c])
    P.op("pool", mset(ones_f, 1.0), writes=[b_c])
    P.op("pool", mset(epsb, EPS), writes=[b_c])
    P.op("pool", mset(maskA, 0.0), writes=[b_mA])
    P.op("pool", lambda e: e.affine_select(out=maskA[:, 0:128], in_=maskA[:, 0:128], compare_op=ALU.is_ge, fill=-30000.0,
                                           base=0, pattern=[[-1, 128]], channel_multiplier=1), reads=[b_mA], writes=[b_mA])
    P.op("pool", lambda e: e.affine_select(out=maskA[:, 256:384], in_=maskA[:, 256:384], compare_op=ALU.is_ge, fill=-30000.0,
                                           base=0, pattern=[[1, 128]], channel_multiplier=-1), reads=[b_mA], writes=[b_mA])
    P.op("pool", mset(vbuf[:, :, 64:128], 1.0), writes=[b_v[0], b_v[1]])
    for l in range(2):
        P.op("pool", mset(vx[l][:, :, :, 64:128], 1.0), writes=[b_vx[l]])
        P.op("pool", mset(mkT[l], 0.0), writes=[b_mkT[l]])

    P.dma("sp", tab_s, tab_d, writes=[b_c])
    P.dma("sp", smallc, smallc_d.partition_broadcast(128), writes=[b_c, b_lam], grp=b_lam.grp)

    P.op("act", act(esink[:, 0:6], esink[:, 0:6], AF.Exp), reads=[b_c], writes=[b_c])
    jk = hbf[0][:, 0:64]
    P.op("dve", stt(jk, lamt[:, 0:64], 1.0, lamt[:, 64:128], ALU.mult, ALU.mult, accum_out=lams[:, 0:1]),
         reads=[b_lam], writes=[b_hbf[0], b_lam])
    P.op("dve", stt(jk, lamt[:, 128:192], 1.0, lamt[:, 192:256], ALU.mult, ALU.mult, accum_out=lams[:, 1:2]),
         reads=[b_lam], writes=[b_hbf[0], b_lam])
    P.op("act", act(lams[:, 0:2], lams[:, 0:2], AF.Exp), reads=[b_lam], writes=[b_lam])
    P.op("dve", tt(lams[:, 2:3], lams[:, 1:2], lams[:, 0:1], ALU.subtract), reads=[b_lam], writes=[b_lam])
    P.op("dve", ts(lams[:, 2:3], lams[:, 2:3], -LAM_INIT, ALU.add), reads=[b_lam], writes=[b_lam])
    P.op("dve", stt(hbf[0][:, 0:128], smallc[:, 320:448], 1.0 - LAM_INIT, Jm, ALU.mult, ALU.mult, accum_out=gsub),
         reads=[b_lam, b_c], writes=[b_hbf[0], b_lam])
    neg_lam = lams[:, 2:3]

    P.dma("sp", ft[:, 2:4, :].rearrange("p a b -> p (a b)"), gains_d[1:2, :].partition_broadcast(128),
          writes=[b_ft[2], b_ft[3]], grp=b_ft[2].grp)
    for t in range(NT):
        P.dma("sp", xres[:, t, :], x_d[t * 128:(t + 1) * 128, :], writes=[b_x[t]])

    def build_bias_vector():
        oht_s = qk4_f32[0:32, 0:RB]
        fv = qk4_f32[0:6, 2048:2048 + RB]
        bq = b_qk[0:4]
        P.dma("sp", oht_s, oht_d, writes=bq, grp=b_qk[0].grp)
        for j, (c0, n) in enumerate([(0, 512), (512, 512), (1024, 256)]):
            P.op("pe", mm(psb[j][0:6, 0:n], tab_s, oht_s[:, c0:c0 + n], True, True), reads=[b_c] + bq, writes=[b_ps[j]])
            P.op("dve", cp(fv[:, c0:c0 + n], psb[j][0:6, 0:n]), reads=[b_ps[j]], writes=bq)
        P.dma("sp", scr.ap(), fv, reads=bq, writes=[b_scr])

    def build_U_dma(h):
        Hk = ft[:, 0:3, :].rearrange("p a b -> p (a b)")[:, 0:UW]
        P.dma("sp", Hk, bass.AP(scr, h * RB, [[1, 128], [1, UW]]), reads=[b_scr], writes=[b_ft[0], b_ft[1], b_ft[2]],
              grp=b_ft[0].grp)

    def build_U_flip():
        Hk = ft[:, 0:3, :].rearrange("p a b -> p (a b)")[:, 0:UW]
        for j, (c0, n) in enumerate([(0, 512), (512, 512), (1024, 128)]):
            bk = [7, 0, 1][j]
            P.op("pe", mm(psb[bk][:, 0:n], Jm, Hk[:, c0:c0 + n], True, True),
                 reads=[b_c, b_ft[0], b_ft[1], b_ft[2]], writes=[b_ps[bk]])
            P.op("dve", cp(U[:, c0:c0 + n], psb[bk][:, 0:n]), reads=[b_ps[bk]], writes=[b_U])

    biasbuf = [(biasA, b_bA), (U[:, 0:384], b_U)]
    b_Uhk = mk("Uhk")

    def bias_A_dma(h):
        Hk = U[:, 384:768]
        P.dma("sp", Hk, bass.AP(scr, h * RB + 384, [[1, 128], [1, 384]]), reads=[b_scr], writes=[b_Uhk])

    def bias_A_flip(h):
        Hk = U[:, 384:768]
        bb, b_bb = biasbuf[h % 2]
        P.op("pe", mm(psb[6][:, 0:384], Jm, Hk, True, True), reads=[b_c, b_Uhk], writes=[b_ps[6]])
        for jj in range(3):
            P.op("dve", tt(bb[:, jj * 128:(jj + 1) * 128], psb[6][:, 256 - 128 * jj:384 - 128 * jj],
                           maskA[:, jj * 128:(jj + 1) * 128], ALU.add), reads=[b_ps[6], b_mA], writes=[b_bb])

    wstate = {"slot": 0}

    def load_group(src3d, c0, n):
        s_ = wstate["slot"]
        wstate["slot"] ^= 1
        P.dma("pool", wbuf[s_][:, :, 0:n], src3d[:, :, c0:c0 + n], writes=[b_w[s_]])
        return s_

    psrot = {"i": 0}

    def next_bank():
        bk = [7, 0, 1, 2][psrot["i"] % 4]
        psrot["i"] += 1
        return bk

    def norm_a(src, b_src, gslot, k):
        hb = hbf[k % 2]
        bhb = b_hbf[k % 2]
        g_ap = ft2(gslot)
        sc = ssn[:, k % 2:k % 2 + 1]
        bsc = b_ssn[k % 2]
        P.op("act", act(hb, src, AF.Square, scale=float(D ** -0.5), accum_out=sc),
             reads=b_src, writes=[bhb, bsc])
        P.op("act", act(sc, sc, AF.Ln, bias=epsb), reads=[bsc, b_c], writes=[bsc])
        P.op("act", act(sc, sc, AF.Exp, scale=-0.5), reads=[bsc], writes=[bsc])
        P.op("dve", stt(hb, src, sc, g_ap, ALU.mult, ALU.mult),
             reads=b_src + [bsc] + b_ft2(gslot), writes=[bhb])

    def norm_b(dstT, b_dst, tcol, k):
        hb = hbf[k % 2]
        bhb = b_hbf[k % 2]
        bk = next_bank()
        pv = psb[bk].bitcast(BF16)
        for c in range(NC):
            P.op("pe", lambda e, c=c, pv=pv, hb=hb: e.transpose(pv[:, c * 128:(c + 1) * 128], hb[:, c * 128:(c + 1) * 128], ident),
                 reads=[bhb, b_c], writes=[b_ps[bk]])
        P.op("act", act(dstT[:, :, tcol:tcol + 128], pv.rearrange("p (c t) -> p c t", c=NC), AF.Copy),
             reads=[b_ps[bk]], writes=b_dst)

    def norm_transpose(src, b_src, gslot, dstT, b_dst, tcol, k):
        norm_a(src, b_src, gslot, k)
        norm_b(dstT, b_dst, tcol, k)

    def load_gain(idx, gslot):
        P.dma("sp", ft2(gslot), gains_d[idx:idx + 1, :].partition_broadcast(128), writes=b_ft2(gslot),
              grp=b_ft[2 * gslot].grp)

    memT = qk4[:, 0, :].rearrange("p (c t) -> p c t", c=NC)

    def mem_path(wm_slots, after_l):
        load_gain(0, 1)
        for m in range(2):
            P.dma("sp", ft2(0), mem_d[m * 128:(m + 1) * 128, :], writes=b_ft2(0), grp=b_ft[0].grp)
            norm_transpose(ft2(0), b_ft2(0), 1, memT, [b_qk[0]], m * 128, m)
        for l in range(2):
            w = wbuf[wm_slots[l]]
            bw = b_w[wm_slots[l]]
            for ch in range(2):
                bk = next_bank()
                for c in range(NC):
                    P.op("pe", mm(psb[bk][:, 0:MEM], w[:, c, ch * 128:(ch + 1) * 128], memT[:, c, 0:MEM], c == 0, c == NC - 1),
                         reads=[bw, b_qk[0]], writes=[b_ps[bk]])
                P.op("dve", cp(mkT[l][0:64, 2 * ch, :], psb[bk][0:64, 0:MEM]), reads=[b_ps[bk]], writes=[b_mkT[l]])
                P.op("dve", cp(mkT[l][64:128, 2 * ch + 1, :], psb[bk][64:128, 0:MEM]), reads=[b_ps[bk]], writes=[b_mkT[l]])
            for mt in range(2):
                bk = next_bank()
                for c in range(NC):
                    P.op("pe", mm(psb[bk][:, 0:256], memT[:, c, mt * 128:(mt + 1) * 128], w[:, c, 256:512], c == 0, c == NC - 1),
                         reads=[bw, b_qk[0]], writes=[b_ps[bk]])
                pvw = psb[bk][:, 0:256].rearrange("p (pr s d) -> p pr s d", pr=2, s=2)
                P.op("dve", cp(vx[l][:, :, mt, 0:64], pvw[:, :, 0, :]), reads=[b_ps[bk]], writes=[b_vx[l]])
                P.op("dve", cp(vx[l][:, :, mt, 128:192], pvw[:, :, 1, :]), reads=[b_ps[bk]], writes=[b_vx[l]])
            if mt == 1:
                after_l[l]()

    def phase_norm(gain_idx, preloaded=False):
        if not preloaded:
            load_gain(gain_idx, 1)
        for t in range(NT + 1):
            if t < NT:
                norm_a(xres[:, t, :], [b_x[t]], 1, t)
            if t >= 1:
                norm_b(hT, [b_hTc[(t - 1) // 4]], (t - 1) * 128, t - 1)

    def proj_fm(wslot, col0, evac):
        w = wbuf[wslot]
        for tc in range(4):
            bk = next_bank()
            for c in range(NC):
                P.op("pe", mm(psb[bk], w[:, c, col0:col0 + 128], hT[:, c, tc * 512:(tc + 1) * 512], c == 0, c == NC - 1),
                     wreads=[b_w[wslot]], reads=[b_hTc[tc]], writes=[b_ps[bk]])
            evac(bk, tc)

    def gate_phase(wd, groups, after_half=None):
        slots = []
        for gi, g in enumerate(("gate0", "gate1")):
            slots.append(load_group(wd, *groups[g]))
        for gi in range(2):
            if gi == 1 and after_half is not None:
                after_half()
            for cc in range(4):
                ch = gi * 4 + cc

                def evac(bk, tc, ch=ch):
                    P.op("act", act(sgy[:, ch, tc * 512:(tc + 1) * 512], psb[bk], AF.Silu),
                         reads=[b_ps[bk]], writes=[b_sgy[ch]])
                proj_fm(slots[gi], cc * 128, evac)

    st = {"s": 0, "p": 0}

    def run_stream(steps, sbanks=(0, 1, 2)):
        n = len(steps)
        LA = len(sbanks) - 1
        pending = []
        seq = 0
        for i in range(n + LA):
            if i < n:
                sbk = sbanks[st["s"] % len(sbanks)]
                st["s"] += 1
                pti = st["p"] % 4
                st["p"] += 1
                steps[i]["_s"] = sbk
                steps[i]["_p"] = pti
                steps[i]["qk"](sbk)
                steps[i]["sm"](sbk, pti)
            j = i - LA
            if j >= 0:
                steps[j]["pv"](steps[j]["_p"])
                fin = steps[j].get("fin")
                if fin is not None:
                    stages = fin if isinstance(fin, list) else [(1, fin)]
                    for dly, fn in stages:
                        pending.append((i + dly, seq, fn))
                        seq += 1
            due = sorted([p for p in pending if p[0] <= i])
            pending = [p for p in pending if p[0] > i]
            for _, _, fn in due:
                fn()
        for _, _, fn in sorted(pending):
            fn()

    fts = {"i": 0}

    def next_ft():
        i = fts["i"] % 4
        fts["i"] += 1
        return i

    def finalize_half(obank, half, ch, q0, nq, esink_col=None, dve_recip=False):
        ro = slice(64 * half, 64 * half + 64)
        rd = slice(64 * (1 - half), 64 * (1 - half) + 64)
        ps = psb[obank]
        fa = next_ft()
        fb = next_ft()
        A_ = ft[:, fa, :]
        B_ = ft[:, fb, :]
        if dve_recip:
            P.op("dve", rcp(B_[ro, 0:nq], ps[rd, 0:nq]), reads=[b_ps[obank]], writes=[b_ft[fb]])
        elif esink_col is not None:
            P.op("act", act(A_[rd, 0:nq], ps[rd, 0:nq], AF.Ln, bias=esink[rd, esink_col:esink_col + 1]),
                 reads=[b_ps[obank], b_c], writes=[b_ft[fa]])
        else:
            P.op("act", act(A_[rd, 0:nq], ps[rd, 0:nq], AF.Ln), reads=[b_ps[obank]], writes=[b_ft[fa]])
        if not dve_recip:
            P.op("act", act(B_[ro, 0:nq], A_[rd, 0:nq], AF.Exp, scale=-1.0), reads=[b_ft[fa]], writes=[b_ft[fb]])
        P.op("dve", tt(A_[ro, 0:nq], ps[ro, 0:nq], B_[ro, 0:nq], ALU.mult),
             reads=[b_ps[obank], b_ft[fb]], writes=[b_ft[fa]])
        dst = sgy[ro, ch, q0:q0 + nq]
        P.op("dve", tt(dst, A_[ro, 0:nq], dst, ALU.mult), reads=[b_ft[fa], b_sgy[ch]], writes=[b_sgy[ch]])

    def dense_steps(qT_ap, b_q, kT_ap, b_k, lhsT_v, b_vv, nkb, obank_of, fin_of, kcol=128):
        steps = []
        for qc in range(4):
            for kb in range(nkb):
                ob = obank_of(qc)

                def qk(sbk, qc=qc, kb=kb):
                    P.op("pe", mm(psb[sbk], kT_ap[:, kb * kcol:(kb + 1) * kcol], qT_ap[:, qc * 512:(qc + 1) * 512], True, True),
                         wreads=[b_k], reads=[b_q], writes=[b_ps[sbk]])

                def sm(sbk, pti):
                    P.op("act", act(pt[pti], psb[sbk], AF.Exp, scale=0.125), reads=[b_ps[sbk]], writes=[b_pt[pti]])

                def pv(pti, kb=kb, ob=ob):
                    P.op("pe", mm(psb[ob], lhsT_v(kb), pt[pti], kb == 0, kb == nkb - 1),
                         wreads=[b_vv], reads=[b_pt[pti]], writes=[b_ps[ob]])
                stp = dict(qk=qk, sm=sm, pv=pv)
                if kb == nkb - 1:
                    stp["fin"] = (lambda qc=qc, ob=ob: fin_of(qc, ob))
                steps.append(stp)
        return steps

    def x_attention(l):
        steps = []
        for xh in range(4):
            ch = xh // 2
            half = xh % 2
            steps += dense_steps(
                qk4[:, ch, :], b_qk[ch], mkT[l][:, xh, :], b_mkT[l],
                lambda kb, ch=ch, half=half: vx[l][:, ch, kb, 64 * half:64 * half + 128], b_vx[l], 2,
                lambda qc, xh=xh: 3 + ((xh * 4 + qc) % 4),
                lambda qc, ob, ch=ch, half=half: finalize_half(ob, half, 6 + ch, qc * 512, 512))
        run_stream(steps)

    def out_proj(wd, final, next_gain=None):
        slots = [load_group(wd, 0, 512), load_group(wd, 512, 512)]
        load_gain(3 if final else next_gain, 1)
        for t in range(NT):
            for half in range(2):
                w = wbuf[slots[half]]
                bk = next_bank()
                for c in range(NC):
                    P.op("pe", mm(psb[bk], sgy[:, c, t * 128:(t + 1) * 128], w[:, c, :], c == 0, c == NC - 1),
                         wreads=[b_sgy[c]], reads=[b_w[slots[half]]], writes=[b_ps[bk]])
                xs_ = xres[:, t, half * 512:(half + 1) * 512]
                P.op("dve", tt(xs_, psb[bk], xs_, ALU.add), reads=[b_ps[bk], b_x[t]], writes=[b_x[t]])
            if final:
                src = xres[:, t, :]
                hb = hbf[t % 2]
                sc = ssn[:, t % 2:t % 2 + 1]
                bsc = b_ssn[t % 2]
                P.op("act", act(hb, src, AF.Square, scale=float(D ** -0.5), accum_out=sc),
                     reads=[b_x[t]], writes=[b_hbf[t % 2], bsc])
                P.op("act", act(sc, sc, AF.Ln, bias=epsb), reads=[bsc, b_c], writes=[bsc])
                P.op("act", act(sc, sc, AF.Exp, scale=-0.5), reads=[bsc], writes=[bsc])
                P.op("dve", stt(src, src, sc, ft2(1), ALU.mult, ALU.mult),
                     reads=[b_x[t], bsc] + b_ft2(1), writes=[b_x[t]])
                P.dma("sp", out_d[t * 128:(t + 1) * 128, :], src, reads=[b_x[t]], writes=[b_out[t]])
            else:
                norm_a(xres[:, t, :], [b_x[t]], 1, t)
                if t >= 1:
                    norm_b(hT, [b_hTc[(t - 1) // 4]], (t - 1) * 128, t - 1)
        if not final:
            norm_b(hT, [b_hTc[3]], (NT - 1) * 128, NT - 1)

    class _Stop(Exception):
        pass

    def checkpoint(name):
        if stop_after == name + "dump":
            xflat = xres.rearrange("p t n -> p (t n)")
            for c in range(NC):
                P.op("dve", cp(xflat[:, c * S:(c + 1) * S], sgy[:, c, :]), reads=[b_sgy[c]] + b_x, writes=b_x)
        if stop_after in (name, name + "dump"):
            for t in range(NT):
                P.dma("sp", out_d[t * 128:(t + 1) * 128, :], xres[:, t, :], reads=[b_x[t]], writes=[b_out[t]])
            raise _Stop()

    def body():
        checkpoint("setup")
        phase_norm(1, preloaded=True)
        checkpoint("norm0")
        wsl = {}
        gate_phase(win_d[0], G0, after_half=lambda: wsl.__setitem__("wm0", load_group(wmem_d[0], 0, 512)))
        checkpoint("gate0")
        build_bias_vector()
        wsl["wm1"] = load_group(wmem_d[1], 0, 512)
        mem_path([wsl["wm0"], wsl["wm1"]],
                 [lambda: wsl.__setitem__("aqk", load_group(win_d[0], *G0["aqk"])),
                  lambda: wsl.__setitem__("av", load_group(win_d[0], *G0["av"]))])

        s_aqk = wsl["aqk"]
        s_av = wsl["av"]
        P.op("pool", mset(qk4[64:128, 3, :], 0.0), writes=[b_qk[3]])
        P.op("pool", mset(qk4[0:64, 4, :], 0.0), writes=[b_qk[4]])
        for j in range(4):
            def evac(bk, tc, j=j):
                if j < 3:
                    P.op("dve", cp(qk4[:, j, tc * 512:(tc + 1) * 512], psb[bk]), reads=[b_ps[bk]], writes=[b_qk[j]])
                else:
                    P.op("dve", cp(qk4[0:64, 3, tc * 512:(tc + 1) * 512], psb[bk][0:64, :]), reads=[b_ps[bk]], writes=[b_qk[3]])
                    P.op("dve", cp(qk4[64:128, 4, tc * 512:(tc + 1) * 512], psb[bk][64:128, :]), reads=[b_ps[bk]], writes=[b_qk[4]])
            proj_fm(s_aqk, j * 128, evac)

        def proj_v_tm(wslot, dst_lo, dst_hi):
            w = wbuf[wslot]
            for t in range(NT):
                bk = next_bank()
                for c in range(NC):
                    P.op("pe", mm(psb[bk][:, 0:128], hT[:, c, t * 128:(t + 1) * 128], w[:, c, 0:128], c == 0, c == NC - 1),
                         reads=[b_w[wslot], b_hTc[t // 4]], writes=[b_ps[bk]])
                P.op("dve", cp(vbuf[:, t, 0:64], psb[bk][:, 0:64]), reads=[b_ps[bk]], writes=[b_v[0], b_v[1]])
                P.op("dve", cp(vbuf[:, t, 128:192], psb[bk][:, 64:128]), reads=[b_ps[bk]], writes=[b_v[0], b_v[1]])
        proj_v_tm(s_av, 0, 128)

        def a_head_steps(h):
            j = h % 3
            half = h // 3
            rows = slice(64 * half, 64 * half + 64)
            steps = []
            for qb in range(NT):
                kbs = [kb for kb in (qb - 1, qb, qb + 1) if 0 <= kb < NT]
                c0 = (kbs[0] - qb + 1) * 128
                n = len(kbs) * 128
                ob = 3 + (qb // 4) % 3
                bb, b_bb = biasbuf[h % 2]

                def qk(sbk, qb=qb, kbs=kbs):
                    for kb in kbs:
                        jj = kb - qb + 1
                        P.op("pe", mm(psb[sbk][:, jj * 128:(jj + 1) * 128], qk4[:, 3 + half, kb * 128:(kb + 1) * 128],
                                      qk4[:, j, qb * 128:(qb + 1) * 128], True, True),
                             wreads=[b_qk[3 + half]], reads=[b_qk[j]], writes=[b_ps[sbk]])

                def sm(sbk, pti, c0=c0, n=n):
                    P.op("dve", stt(psb[sbk][:, c0:c0 + n], psb[sbk][:, c0:c0 + n], 0.125, bb[:, c0:c0 + n], ALU.mult, ALU.add),
                         reads=[b_ps[sbk], b_bb], writes=[b_ps[sbk]])
                    P.op("act", act(pt[pti][:, c0:c0 + n], psb[sbk][:, c0:c0 + n], AF.Exp), reads=[b_ps[sbk]], writes=[b_pt[pti]])

                def pv(pti, qb=qb, kbs=kbs, ob=ob):
                    for idx, kb in enumerate(kbs):
                        jj = kb - qb + 1
                        P.op("pe", mm(psb[ob][:, (qb % 4) * 128:(qb % 4) * 128 + 128], vbuf[:, kb, 64 * half:64 * half + 128],
                                      pt[pti][:, jj * 128:(jj + 1) * 128], idx == 0, idx == len(kbs) - 1),
                             wreads=[b_v[0]], reads=[b_pt[pti]], writes=[b_ps[ob]])
                stp = dict(qk=qk, sm=sm, pv=pv)
                if qb % 4 == 3:
                    stp["fin"] = [(1, (lambda qc=qb // 4, ob=ob: finalize_half(ob, half, j, qc * 512, 512, esink_col=h)))]
                if qb == 5 and h + 1 < 6:
                    stp["fin"] = [(0, (lambda: bias_A_flip(h + 1)))]
                steps.append(stp)
            return steps

        checkpoint("aproj")
        bias_A_dma(0)
        bias_A_flip(0)
        for h in range(6):
            if h + 1 < 6:
                bias_A_dma(h + 1)
            run_stream(a_head_steps(h), sbanks=(0, 1, 2, 7))
        checkpoint("A")

        s_bqk = load_group(win_d[0], *G0["bqk"])
        s_bv = load_group(win_d[0], *G0["bv"])
        P.dma("sp", ft[:, 3, :], qkn_d.partition_broadcast(128), writes=[b_ft[3]])
        qkn_t = psb[6]
        P.op("dve", cp(qkn_t, ft[:, 3, :]), reads=[b_ft[3]], writes=[b_ps[6]])
        xs_of = lambda t: (ft[:, 0, :], b_ft[0]) if t % 2 == 0 else (ft[:, 3, :], b_ft[3])
        b_ss2 = [b_ss, b_ssn[0]]
        ss_of = lambda t: (ss[:, 0:8], b_ss) if t % 2 == 0 else (ss8b, b_ssn[0])

        def b_stage1(t):
            rs = ropes[t % 2]
            P.dma("sp", rs, rope_d[t * 128:(t + 1) * 128, :], writes=[b_rope[t % 2]])
            bk = [0, 1][t % 2]
            for c in range(NC):
                P.op("pe", mm(psb[bk], hT[:, c, t * 128:(t + 1) * 128], wbuf[s_bqk][:, c, :], c == 0, c == NC - 1),
                     reads=[b_w[s_bqk], b_hTc[t // 4]], writes=[b_ps[bk]])
            bk2 = [2, 7][t % 2]
            for c in range(NC):
                P.op("pe", mm(psb[bk2][:, 0:128], hT[:, c, t * 128:(t + 1) * 128], wbuf[s_bv][:, c, 0:128], c == 0, c == NC - 1),
                     reads=[b_w[s_bv], b_hTc[t // 4]], writes=[b_ps[bk2]])
            xs_, bxs = xs_of(t)
            P.op("act", act(xs_, psb[bk], AF.Copy), reads=[b_ps[bk]], writes=[bxs])
            P.op("act", act(vbuf[:, t, 0:64], psb[bk2][:, 0:64], AF.Copy), reads=[b_ps[bk2]], writes=[b_v[0], b_v[1]])
            P.op("act", act(vbuf[:, t, 128:192], psb[bk2][:, 64:128], AF.Copy), reads=[b_ps[bk2]], writes=[b_v[0], b_v[1]])

        def b_stage2a(t):
            xs_, bxs = xs_of(t)
            ssv, bssv = ss_of(t)
            sqp = psb[5]
            P.op("dve", tt(sqp, xs_, xs_, ALU.mult), reads=[bxs], writes=[b_ps[5]])
            P.op("dve", lambda e, ssv=ssv: e.tensor_reduce(out=ssv, in_=sqp.rearrange("p (h d) -> p h d", h=8), axis=AX.X, op=ALU.add),
                 reads=[b_ps[5]], writes=[bssv])
            P.op("act", act(ssv, ssv, AF.Ln, scale=1.0 / 64, bias=epsb), reads=[bssv, b_c], writes=[bssv])
            P.op("act", act(ssv, ssv, AF.Exp, scale=-0.5), reads=[bssv], writes=[bssv])

        def b_stage2b(t):
            rs = ropes[t % 2]
            xs_, bxs = xs_of(t)
            ssv, bssv = ss_of(t)
            sq_ = ft[:, 1, :]
            t2_ = ft[:, 2, :]
            x3 = xs_.rearrange("p (h d) -> p h d", h=8)
            P.op("dve", tt(x3, x3, ssv.unsqueeze(2).to_broadcast([128, 8, 64]), ALU.mult),
                 reads=[bxs, bssv], writes=[bxs])
            P.op("dve", tt(xs_, xs_, qkn_t, ALU.mult), reads=[bxs, b_ps[6]], writes=[bxs])
            cc = rs[:, 0:64].unsqueeze(1).to_broadcast([128, 8, 64])
            P.op("dve", tt(sq_.rearrange("p (h d) -> p h d", h=8), x3, cc, ALU.mult),
                 reads=[bxs, b_rope[t % 2]], writes=[b_ft[1]])
            x4 = xs_.rearrange("p (h a s d) -> p h a s d", h=8, a=2, s=2)
            t4 = t2_.rearrange("p (h a s d) -> p h a s d", h=8, a=2, s=2)
            s4 = rs[:, 64:128].rearrange("p (a s d) -> p a s d", a=2, s=2)
            for s_ in range(2):
                P.op("dve", tt(t4[:, :, :, s_, :], x4[:, :, :, 1 - s_, :],
                               s4[:, :, s_, :].unsqueeze(1).to_broadcast([128, 8, 2, 16]), ALU.mult),
                     reads=[bxs, b_rope[t % 2]], writes=[b_ft[2]])
            hb = hbf[t % 2]
            P.op("dve", tt(hb[:, 0:512], sq_, t2_, ALU.add), reads=[b_ft[1], b_ft[2]], writes=[b_hbf[t % 2]])

        def b_stage3(t):
            hb = hbf[t % 2]
            bk3 = [3, 4][t % 2]
            pvv = psb[bk3].bitcast(BF16)
            for c in range(4):
                P.op("pe", lambda e, c=c, pvv=pvv, hb=hb: e.transpose(pvv[:, c * 128:(c + 1) * 128], hb[:, c * 128:(c + 1) * 128], ident),
                     reads=[b_hbf[t % 2], b_c], writes=[b_ps[bk3]])
            P.op("act", act(qk4[:, 0:3, t * 128:(t + 1) * 128], pvv[:, 0:384].rearrange("p (c t) -> p c t", c=3), AF.Copy),
                 reads=[b_ps[bk3]], writes=b_qk[0:3])
            P.op("act", act(qk4[0:64, 3, t * 128:(t + 1) * 128], pvv[0:64, 384:512], AF.Copy), reads=[b_ps[bk3]], writes=[b_qk[3]])
            P.op("act", act(qk4[64:128, 4, t * 128:(t + 1) * 128], pvv[64:128, 384:512], AF.Copy), reads=[b_ps[bk3]], writes=[b_qk[4]])

        b_stage1(0)
        b_stage2a(0)
        for t in range(NT):
            if t + 1 < NT:
                b_stage1(t + 1)
            b_stage2b(t)
            if t + 1 < NT:
                b_stage2a(t + 1)
            b_stage3(t)

        checkpoint("bprep")
        steps = []
        for h in range(6):
            j = h % 3
            half = h // 3
            steps += dense_steps(
                qk4[:, j, :], b_qk[j], qk4[:, 3 + half, :], b_qk[3 + half],
                lambda kb, half=half: vbuf[:, kb, 64 * half:64 * half + 128], b_v[0], NT,
                lambda qc, h=h: 3 + ((h * 4 + qc) % 4),
                lambda qc, ob, j=j, half=half: finalize_half(ob, half, 3 + j, qc * 512, 512, dve_recip=True))
        run_stream(steps)
        checkpoint("B")

        def x_proj(wd, grp):
            s_x = load_group(wd, *grp)
            for ch in range(2):
                def evac(bk, tc, ch=ch):
                    P.op("dve", cp(qk4[:, ch, tc * 512:(tc + 1) * 512], psb[bk]), reads=[b_ps[bk]], writes=[b_qk[ch]])
                proj_fm(s_x, ch * 128, evac)

        x_proj(win_d[0], G0["xq"])
        x_attention(0)
        checkpoint("X0")
        out_proj(wout_d[0], final=(n_layers == 1), next_gain=2)
        checkpoint("out0")

        if n_layers == 2:
            gate_phase(win_d[1], G1)
            for h in range(6):
                build_U_dma(h)
                s_c = load_group(win_d[1], *G1["c%d" % h])
                qi = 0 if h % 2 == 0 else 3
                if h == 0:
                    P.op("pool", mset(qk4[64:128, 1, :], 0.0), writes=[b_qk[1]])
                    P.op("pool", mset(qk4[0:64, 2, :], 0.0), writes=[b_qk[2]])
                vsl = slice((h % 2) * 128, (h % 2) * 128 + 128)
                bvh = b_v[h % 2]
                def evac_q(bk, tc):
                    P.op("dve", cp(qk4[:, qi, tc * 512:(tc + 1) * 512], psb[bk]), reads=[b_ps[bk]], writes=[b_qk[qi]])

                def evac_k(bk, tc):
                    P.op("dve", cp(qk4[0:64, 1, tc * 512:(tc + 1) * 512], psb[bk][0:64, :]), reads=[b_ps[bk]], writes=[b_qk[1]])
                    P.op("dve", cp(qk4[64:128, 2, tc * 512:(tc + 1) * 512], psb[bk][64:128, :]), reads=[b_ps[bk]], writes=[b_qk[2]])
                proj_fm(s_c, 0, evac_q)
                proj_fm(s_c, 128, evac_k)
                for t in range(NT):
                    bk = next_bank()
                    for c in range(NC):
                        P.op("pe", mm(psb[bk][:, 0:128], hT[:, c, t * 128:(t + 1) * 128], wbuf[s_c][:, c, 256:384], c == 0, c == NC - 1),
                             reads=[b_w[s_c], b_hTc[t // 4]], writes=[b_ps[bk]])
                    P.op("act", act(vbuf[:, t, vsl], psb[bk][:, 0:128], AF.Copy), reads=[b_ps[bk]], writes=[bvh])
                build_U_flip()
                steps = []
                for qc in range(4):
                    for s_ in range(2):
                        rows = slice(64 * s_, 64 * s_ + 64)
                        ob, db = (3, 4) if s_ == 0 else (5, 6)
                        for kb in range(NT):
                            delta = kb - 4 * qc

                            def qk(sbk, kb=kb, qc=qc, s_=s_):
                                P.op("pe", mm(psb[sbk], qk4[:, 1 + s_, kb * 128:(kb + 1) * 128], qk4[:, qi, qc * 512:(qc + 1) * 512], True, True),
                                     wreads=[b_qk[1 + s_]], reads=[b_qk[qi]], writes=[b_ps[sbk]])

                            def sm(sbk, pti, delta=delta):
                                if -1 <= delta <= 4:
                                    m0 = 512 - 128 * delta
                                    P.op("dve", stt(psb[sbk], psb[sbk], 0.125, U[:, m0:m0 + 512], ALU.mult, ALU.add),
                                         reads=[b_ps[sbk], b_U], writes=[b_ps[sbk]])
                                    P.op("act", act(pt[pti], psb[sbk], AF.Exp), reads=[b_ps[sbk]], writes=[b_pt[pti]])
                                else:
                                    col = UW - 1 if delta <= -2 else 0
                                    P.op("act", act(pt[pti], psb[sbk], AF.Exp, scale=0.125, bias=U[:, col:col + 1]),
                                         reads=[b_ps[sbk], b_U], writes=[b_pt[pti]])

                            def pv(pti, kb=kb, ob=ob, db=db):
                                P.op("pe", mm(psb[ob], vbuf[:, kb, vsl], pt[pti], kb == 0, kb == NT - 1),
                                     wreads=[bvh], reads=[b_pt[pti]], writes=[b_ps[ob]])
                                P.op("pe", mm(psb[db], ones_bf, pt[pti], kb == 0, kb == NT - 1),
                                     wreads=[b_c], reads=[b_pt[pti]], writes=[b_ps[db]])
                            stp = dict(qk=qk, sm=sm, pv=pv)
                            if kb == NT - 1:
                                if s_ == 0:
                                    def fin(ob=ob, db=db, qc=qc):
                                        if qc >= 2:
                                            P.op("dve", rcp(ft[:, 0, :], psb[db]), reads=[b_ps[db]], writes=[b_ft[0]])
                                        else:
                                            P.op("act", act(ft[:, 0, :], psb[db], AF.Ln), reads=[b_ps[db]], writes=[b_ft[0]])
                                            P.op("act", act(ft[:, 0, :], ft[:, 0, :], AF.Exp, scale=-1.0), reads=[b_ft[0]], writes=[b_ft[0]])
                                        P.op("dve", tt(ft[:, 1, :], psb[ob], ft[:, 0, :], ALU.mult),
                                             reads=[b_ps[ob], b_ft[0]], writes=[b_ft[1]])
                                else:
                                    r2, t1, d_, q_ = ft[:, 0, :], ft[:, 1, :], ft[:, 2, :], ft[:, 3, :]

                                    def fin(ob=ob, db=db, qc=qc):
                                        if qc >= 2:
                                            P.op("dve", rcp(r2, psb[db]), reads=[b_ps[db]], writes=[b_ft[0]])
                                        else:
                                            P.op("act", act(r2, psb[db], AF.Ln), reads=[b_ps[db]], writes=[b_ft[0]])
                                            P.op("act", act(r2, r2, AF.Exp, scale=-1.0), reads=[b_ft[0]], writes=[b_ft[0]])
                                        P.op("dve", tt(d_, psb[ob], r2, ALU.mult), reads=[b_ps[ob], b_ft[0]], writes=[b_ft[2]])
                                        P.op("dve", stt(d_, d_, neg_lam, t1, ALU.mult, ALU.add),
                                             reads=[b_ft[2], b_ft[1], b_lam], writes=[b_ft[2]])
                                        P.op("dve", tt(q_, d_, d_, ALU.mult), reads=[b_ft[2]], writes=[b_ft[3]])

                                    def finB(db=db):
                                        P.op("pe", mm(psb[db], ones_f, q_, True, True), reads=[b_c, b_ft[3]], writes=[b_ps[db]])
                                        P.op("act", act(q_, psb[db], AF.Ln, scale=1.0 / 128, bias=epsb),
                                             reads=[b_ps[db], b_c], writes=[b_ft[3]])
                                        P.op("act", act(q_, q_, AF.Exp, scale=-0.5), reads=[b_ft[3]], writes=[b_ft[3]])

                                    def finC(qc=qc, h=h):
                                        P.op("dve", tt(d_, d_, q_, ALU.mult), reads=[b_ft[2], b_ft[3]], writes=[b_ft[2]])
                                        dst = sgy[:, h, qc * 512:(qc + 1) * 512]
                                        P.op("dve", stt(dst, d_, gsub, dst, ALU.mult, ALU.mult),
                                             reads=[b_ft[2], b_lam, b_sgy[h]], writes=[b_sgy[h]])
                                stp["fin"] = [(1, fin)] if s_ == 0 else [(1, fin), (9, finB), (12, finC)]
                            steps.append(stp)
                run_stream(steps, sbanks=(0, 1, 2, 7))
            x_proj(win_d[1], G1["xq"])
            x_attention(1)
            out_proj(wout_d[1], final=True)


    try:
        body()
    except _Stop:
        pass
    stats = P.emit()
    return nc, stats


_CACHE = {}


def _prep_weights(inputs):
    f = lambda a: np.asarray(a, dtype=np.float32)
    oht, rope = _host_consts()
    w_in0 = _pc(f(inputs["even_w_in"])[0][:, _cols0()])
    w_in1 = _pc(f(inputs["odd_w_in"])[0][:, _cols1()])
    w_out0 = _pc(f(inputs["even_w_out"])[0][MYORDER0, :])
    w_out1 = _pc(f(inputs["odd_w_out"])[0])
    w_mem0 = _pc(f(inputs["even_w_mem_kv"])[0])
    w_mem1 = _pc(f(inputs["odd_w_mem_kv"])[0])
    gains = np.stack([f(inputs["mem_norm"]), f(inputs["even_norm"])[0], f(inputs["odd_norm"])[0],
                      f(inputs["final_norm"])]).astype(np.float32)
    qn, kn = f(inputs["even_q_norm"])[0], f(inputs["even_k_norm"])[0]
    qkn = np.concatenate([np.tile(qn, 6), np.tile(kn, 2)])[None, :].astype(np.float32)
    lamv = np.concatenate([f(inputs["odd_lambda_q1"])[0], f(inputs["odd_lambda_k1"])[0],
                           f(inputs["odd_lambda_q2"])[0], f(inputs["odd_lambda_k2"])[0]])[None, :].astype(np.float32)
    smallc = np.zeros((1, 512), np.float32)
    smallc[0, 0:6] = f(inputs["even_sink"])[0]
    smallc[0, 64:320] = lamv[0]
    smallc[0, 320:448] = f(inputs["odd_subln"])[0][::-1]
    return dict(tab=np.ascontiguousarray(f(inputs["rel_bias"])), oht=oht, rope=rope, gains=gains,
                w_in0=w_in0, w_in1=w_in1, w_mem0=w_mem0, w_mem1=w_mem1, w_out0=w_out0, w_out1=w_out1,
                smallc=smallc, qkn=qkn)


def kernel(**inputs):
    if "nc" not in _CACHE:
        _CACHE["nc"], _CACHE["stats"] = build_program()
    nc = _CACHE["nc"]
    shared = _prep_weights(inputs)
    x = np.asarray(inputs["x"], dtype=np.float32)
    mem = np.asarray(inputs["mem"], dtype=np.float32)
    in_maps = []
    for b in range(8):
        m = dict(shared)
        m["x"] = np.ascontiguousarray(x[b])
        m["mem"] = np.ascontiguousarray(mem[b])
        in_maps.append(m)
    res = run_bass_kernel_spmd(nc, in_maps, core_ids=list(range(8)))
    return np.stack([np.asarray(r["out"], dtype=np.float32) for r in res.results], axis=0)
```
